# Optimizing a Trainium2 kernel written in Bass

```python
import math
import jax, jax.numpy as jnp
from jax import lax
import numpy as np

D_MODEL = 2048
BATCH = 4
SEQ = 8192
DEPTH = 1

MIX_WIDTH = D_MODEL
HEAD_DIM = 128
A_WIDTH = MIX_WIDTH // 2
A_HEADS = A_WIDTH // HEAD_DIM
A_KV_HEADS = 2
WINDOW = 128
BLOCK = 128
B_WIDTH = MIX_WIDTH - A_WIDTH
B_QK_DIM = 64
B_V_DIM = 2 * B_QK_DIM
B_HEADS = B_WIDTH // B_V_DIM
N_ATTN_HEADS = A_HEADS + B_HEADS
D_FF = -(-8 * D_MODEL // (3 * 256)) * 256
N_MOD = 6
EPS = 1e-6
NEG_INF = -1e30

A_Q_COLS = A_HEADS * HEAD_DIM
A_KV_COLS = A_KV_HEADS * HEAD_DIM
B_Q_COLS = B_HEADS * 2 * B_QK_DIM
B_K_COLS = B_HEADS * 2 * B_QK_DIM
B_V_COLS = B_HEADS * B_V_DIM
IN_COLS = A_Q_COLS + 2 * A_KV_COLS + B_Q_COLS + B_K_COLS + B_V_COLS

kernel_name = "hymba_style_window_gqa_diff_attn_swiglu_block"


def rms_norm(x, gain):
    xf = x.astype(jnp.float32)
    y = xf * lax.rsqrt(jnp.mean(xf * xf, axis=-1, keepdims=True) + EPS)
    return (y * gain.astype(jnp.float32)).astype(x.dtype)


def alibi_slopes(n):
    return 2.0 ** (-8.0 * jnp.arange(1, n + 1, dtype=jnp.float32) / n)


def lambda_init_fn(layer_idx):
    return 0.8 - 0.6 * math.exp(-0.3 * layer_idx)


def windowed_gqa(q, k, v, sink, slopes):
    b_, s_, _, dh = q.shape
    nb = s_ // BLOCK
    g_ = A_HEADS // A_KV_HEADS
    qb = q.reshape(b_, nb, BLOCK, A_KV_HEADS, g_, dh)
    pad = ((0, 0), (BLOCK, BLOCK), (0, 0), (0, 0))
    kp = jnp.pad(k, pad).reshape(b_, nb + 2, BLOCK, A_KV_HEADS, dh)
    vp = jnp.pad(v, pad).reshape(b_, nb + 2, BLOCK, A_KV_HEADS, dh)
    kb = jnp.concatenate([kp[:, :-2], kp[:, 1:-1], kp[:, 2:]], axis=2)
    vb = jnp.concatenate([vp[:, :-2], vp[:, 1:-1], vp[:, 2:]], axis=2)
    scores = jnp.einsum('bnqhgd,bnshd->bnhgqs', qb, kb).astype(jnp.float32) * (dh ** -0.5)
    qpos = jnp.arange(BLOCK)[:, None] + BLOCK
    kpos = jnp.arange(3 * BLOCK)[None, :]
    dist = jnp.abs(qpos - kpos)
    kabs = jnp.arange(nb)[:, None] * BLOCK - BLOCK + kpos
    valid = (dist <= WINDOW)[None] & ((kabs >= 0) & (kabs < s_))[:, None, :]
    bias = -slopes.reshape(A_KV_HEADS, g_)[:, :, None, None] * dist.astype(jnp.float32)
    scores = jnp.where(valid[None, :, None, None], scores + bias[None, None], NEG_INF)
    sink_b = sink.astype(jnp.float32).reshape(1, 1, A_KV_HEADS, g_, 1, 1)
    m = jnp.maximum(jnp.max(scores, axis=-1, keepdims=True), sink_b)
    p = jnp.exp(scores - m)
    probs = p / (jnp.sum(p, axis=-1, keepdims=True) + jnp.exp(sink_b - m))
    out = jnp.einsum('bnhgqs,bnshd->bnqhgd', probs.astype(v.dtype), vb)
    return out.reshape(b_, s_, A_HEADS * dh)


def diff_attention(q, k, v, lam, slopes):
    b_, s_, h_, _, dq = q.shape
    nb = s_ // BLOCK
    qblocks = jnp.moveaxis(q.reshape(b_, nb, BLOCK, h_, 2, dq), 1, 0)
    kpos = jnp.arange(s_)

    def one_block(args):
        qb, i = args
        sc = jnp.einsum('bqhcd,bshcd->bhcqs', qb, k).astype(jnp.float32) * (dq ** -0.5)
        qpos = i * BLOCK + jnp.arange(BLOCK)
        dist = jnp.abs(qpos[:, None] - kpos[None, :]).astype(jnp.float32)
        sc = sc - slopes[None, :, None, None, None] * dist[None, None, None]
        p = jax.nn.softmax(sc, axis=-1)
        w = p[:, :, 0] - lam * p[:, :, 1]
        return jnp.einsum('bhqs,bshd->bqhd', w.astype(v.dtype), v)

    out = lax.map(one_block, (qblocks, jnp.arange(nb)))
    return jnp.moveaxis(out, 0, 1).reshape(b_, s_, h_, v.shape[-1])


def setup_inputs(seed: int = 0) -> dict:
    key = jax.random.key(seed)
    ks = jax.random.split(key, 20)
    f32 = jnp.float32
    nrm = lambda k, shape, scale: jax.random.normal(k, shape, f32) * scale
    gain = lambda k, shape: 1.0 + 0.02 * jax.random.normal(k, shape, f32)
    return {
        "x": jax.random.normal(ks[0], (BATCH, SEQ, D_MODEL), f32),
        "c": jax.random.normal(ks[1], (BATCH, D_MODEL), f32),
        "w_ada": nrm(ks[2], (DEPTH, D_MODEL, N_MOD * D_MODEL), 0.5 * D_MODEL ** -0.5),
        "b_ada": nrm(ks[3], (DEPTH, N_MOD * D_MODEL), 0.01),
        "norm1_gain": gain(ks[4], (DEPTH, D_MODEL)),
        "w_in": nrm(ks[5], (DEPTH, D_MODEL, IN_COLS), D_MODEL ** -0.5),
        "a_sink": nrm(ks[6], (DEPTH, A_HEADS), 0.5),
        "a_out_gain": gain(ks[7], (DEPTH, A_WIDTH)),
        "diff_lq1": nrm(ks[8], (DEPTH, B_QK_DIM), 0.1),
        "diff_lk1": nrm(ks[9], (DEPTH, B_QK_DIM), 0.1),
        "diff_lq2": nrm(ks[10], (DEPTH, B_QK_DIM), 0.1),
        "diff_lk2": nrm(ks[11], (DEPTH, B_QK_DIM), 0.1),
        "diff_subln_gain": gain(ks[12], (DEPTH, B_V_DIM)),
        "w_o": nrm(ks[13], (DEPTH, MIX_WIDTH, D_MODEL), MIX_WIDTH ** -0.5),
        "norm2_gain": gain(ks[14], (DEPTH, D_MODEL)),
        "w_gate": nrm(ks[15], (DEPTH, D_MODEL, D_FF), D_MODEL ** -0.5),
        "w_up": nrm(ks[16], (DEPTH, D_MODEL, D_FF), D_MODEL ** -0.5),
        "w_down": nrm(ks[17], (DEPTH, D_FF, D_MODEL), D_FF ** -0.5),
        "final_gain": gain(ks[18], (D_MODEL,)),
    }


def reference(x, c, w_ada, b_ada, norm1_gain, w_in, a_sink, a_out_gain,
              diff_lq1, diff_lk1, diff_lq2, diff_lk2, diff_subln_gain, w_o,
              norm2_gain, w_gate, w_up, w_down, final_gain):
    b_, s_, _ = x.shape
    slopes = alibi_slopes(N_ATTN_HEADS)
    slopes_a, slopes_b = slopes[:A_HEADS], slopes[A_HEADS:]
    o1 = A_Q_COLS
    o2 = o1 + A_KV_COLS
    o3 = o2 + A_KV_COLS
    o4 = o3 + B_Q_COLS
    o5 = o4 + B_K_COLS
    for l in range(DEPTH):
        mod = jnp.einsum('bd,de->be', jax.nn.silu(c), w_ada[l]) + b_ada[l]
        sh1, sc1, g1, sh2, sc2, g2 = jnp.split(mod[:, None, :], N_MOD, axis=-1)

        h = rms_norm(x, norm1_gain[l]) * (1.0 + sc1) + sh1
        proj = jnp.einsum('bsd,de->bse', h, w_in[l])
        qa = proj[..., :o1].reshape(b_, s_, A_HEADS, HEAD_DIM)
        ka = proj[..., o1:o2].reshape(b_, s_, A_KV_HEADS, HEAD_DIM)
        va = proj[..., o2:o3].reshape(b_, s_, A_KV_HEADS, HEAD_DIM)
        qd = proj[..., o3:o4].reshape(b_, s_, B_HEADS, 2, B_QK_DIM)
        kd = proj[..., o4:o5].reshape(b_, s_, B_HEADS, 2, B_QK_DIM)
        vd = proj[..., o5:].reshape(b_, s_, B_HEADS, B_V_DIM)

        out_a = rms_norm(windowed_gqa(qa, ka, va, a_sink[l], slopes_a), a_out_gain[l])

        lam_init = lambda_init_fn(l)
        lam = (jnp.exp(jnp.sum(diff_lq1[l].astype(jnp.float32) * diff_lk1[l].astype(jnp.float32)))
               - jnp.exp(jnp.sum(diff_lq2[l].astype(jnp.float32) * diff_lk2[l].astype(jnp.float32)))
               + lam_init)
        od = diff_attention(qd, kd, vd, lam, slopes_b)
        out_b = (rms_norm(od, diff_subln_gain[l]) * (1.0 - lam_init)).reshape(b_, s_, B_WIDTH)

        mix = jnp.einsum('bse,ed->bsd', jnp.concatenate([out_a, out_b], axis=-1), w_o[l])
        x = x + g1 * mix

        h2 = rms_norm(x, norm2_gain[l]) * (1.0 + sc2) + sh2
        ff = jax.nn.silu(jnp.einsum('bsd,df->bsf', h2, w_gate[l])) * jnp.einsum('bsd,df->bsf', h2, w_up[l])
        x = x + g2 * jnp.einsum('bsf,fd->bsd', ff, w_down[l])
    return rms_norm(x, final_gain)
```

```python
import contextlib
import os
import numpy as np
import concourse.bass as bass
import concourse.mybir as mybir
from concourse.bass_utils import run_bass_kernel_spmd

F32 = mybir.dt.float32
BF16 = mybir.dt.bfloat16
AF = mybir.ActivationFunctionType
ALU = mybir.AluOpType
AX = mybir.AxisListType

D = 2048
SEQ = 8192
OWN = 4096
KC = 16
DFF = 5632
FC = 44
EPS = 1e-6
SCALE_A = 128.0 ** -0.5
SCALE_B = 64.0 ** -0.5
LAM_INIT = 0.2
SLOPES = [2.0 ** (-8.0 * h / 16.0) for h in range(1, 17)]
SLOPES_A = SLOPES[:8]
SLOPES_B = SLOPES[8:]
NEG = -30000.0


class Op:
    __slots__ = ("eng", "fn", "deps", "signal", "sigval", "is_dma", "slot")

    def __init__(self, eng, fn):
        self.eng = eng
        self.fn = fn
        self.deps = []
        self.signal = False
        self.sigval = None
        self.is_dma = False
        self.slot = None


class Sched:
    ENGS = ("pe", "act", "dve", "pool", "sp")

    def __init__(self, nc, stack, n_dma_slots=8):
        self.nc = nc
        self.n_slots = n_dma_slots
        self.sems = {}
        for e in self.ENGS:
            self.sems[e] = stack.enter_context(nc.semaphore("s_" + e))
        for e in ("sp", "pool"):
            for s in range(n_dma_slots):
                self.sems[(e, s)] = stack.enter_context(nc.semaphore("d_%s_%d" % (e, s)))
        self.cnt = {k: 0 for k in self.sems}
        self.slot_rr = {e: 0 for e in self.ENGS}
        self.slot_last = {}
        self.out_dma = []
        self._reset()

    def _reset(self):
        self.ops = {e: [] for e in self.ENGS}
        self.last_w = {}
        self.readers = {}

    def op(self, eng, fn, reads=(), writes=(), dma=False, is_output=False):
        o = Op(eng, fn)
        o.is_dma = dma
        deps = []
        for b in reads:
            w = self.last_w.get(b)
            if w is not None:
                deps.append(w)
        for b in writes:
            w = self.last_w.get(b)
            if w is not None:
                deps.append(w)
            deps.extend(self.readers.get(b, ()))
        if dma:
            s = self.slot_rr[eng]
            self.slot_rr[eng] = (s + 1) % self.n_slots
            o.slot = (eng, s)
            prev = self.slot_last.get(o.slot)
            if prev is not None:
                deps.append(prev)
            self.slot_last[o.slot] = o
            if is_output:
                self.out_dma.append(o)
        o.deps = deps
        self.ops[eng].append(o)
        for b in reads:
            lst = self.readers.setdefault(b, [])
            if not dma:
                for i, r in enumerate(lst):
                    if (not r.is_dma) and r.eng == eng:
                        del lst[i]
                        break
            lst.append(o)
        for b in writes:
            self.last_w[b] = o
            self.readers[b] = []
        return o

    def emit(self):
        nc = self.nc
        lasts = []
        for e in self.ENGS:
            comp = [o for o in self.ops[e] if not o.is_dma and o.fn is not None]
            if comp:
                lasts.append(comp[-1])
        lasts.extend([o for o in self.slot_last.values() if o is not None])
        for e in self.ENGS:
            fin = Op(e, None)
            fin.deps = [d for d in lasts]
            self.ops[e].append(fin)
        for e in self.ENGS:
            for o in self.ops[e]:
                for d in o.deps:
                    if d.sigval is not None:
                        continue
                    if d.is_dma:
                        d.signal = True
                    elif d.eng != o.eng or o.is_dma or o.fn is None:
                        d.signal = True
                    elif d.eng != "pe":
                        d.signal = True
        for e in self.ENGS:
            for o in self.ops[e]:
                if o.fn is None or o.sigval is not None:
                    continue
                if o.is_dma:
                    self.cnt[o.slot] += 16
                    o.sigval = self.cnt[o.slot]
                elif o.signal:
                    self.cnt[e] += 1
                    o.sigval = self.cnt[e]
        sems = self.sems

        def stream(e):
            ops = self.ops[e]

            def f(eng):
                seen = {}
                for o in ops:
                    need = {}
                    for d in o.deps:
                        if (not d.is_dma) and d.eng == e and (not o.is_dma) and o.fn is not None and e == "pe":
                            continue
                        key = d.slot if d.is_dma else d.eng
                        if need.get(key, 0) < d.sigval:
                            need[key] = d.sigval
                    for key, v in need.items():
                        if seen.get(key, 0) >= v:
                            continue
                        seen[key] = v
                        eng.wait_ge(sems[key], v)
                    if o.fn is None:
                        continue
                    ins = o.fn(eng)
                    if o.is_dma:
                        ins.then_inc(sems[o.slot], 16)
                    elif o.signal:
                        ins.then_inc(sems[e], 1)
            return f

        with nc.Block() as block:
            block.tensor(stream("pe"))
            block.scalar(stream("act"))
            block.vector(stream("dve"))
            block.gpsimd(stream("pool"))
            block.sync(stream("sp"))
        self._reset()


def build_program(stop_after=99, debug_outputs=False):
    nc = bass.Bass("TRN2", target_bir_lowering=False)

    def din(name, shape, dt=F32):
        return nc.dram_tensor(name, list(shape), dt, kind="ExternalInput").ap()

    def dscr(name, shape, dt=BF16):
        kind = "ExternalOutput" if debug_outputs else "Internal"
        return nc.dram_tensor(name, list(shape), dt, kind=kind).ap()

    xT = din("xT", [128, KC, SEQ])
    cT = din("cT", [128, KC])
    w_ada = din("w_ada", [96, 128, KC, 128])
    b_ada = din("b_ada", [128, 96])
    n1g = din("n1g", [128, KC])
    n2g = din("n2g", [128, KC])
    fgn = din("fgn", [128, KC])
    w_in = din("w_in", [36, 128, KC, 128])
    w_o = din("w_o", [16, 128, KC, 128])
    w_gate = din("w_gate", [FC, 128, KC, 128])
    w_up = din("w_up", [FC, 128, KC, 128])
    w_down = din("w_down", [16, 128, FC, 128])
    sinkb = din("sinkb", [128, 8])
    gainA = din("gainA", [128, 1024])
    gainB = din("gainB", [128, 128])
    lqk = din("lqk", [128, 4, 64])
    c_ident = din("c_ident", [128, 128])
    c_biasA = din("c_biasA", [128, 6, 512])
    c_tdist = din("c_tdist", [128, 896])
    c_biasB = din("c_biasB", [128, 8, 126])
    c_fbfa = din("c_fbfa", [128, 8, 8])
    outT = nc.dram_tensor("outT", [128, KC, OWN], F32, kind="ExternalOutput").ap()

    w_in_bf = dscr("w_in_bf", [36, 128, KC, 128])
    w_o_bf = dscr("w_o_bf", [16, 128, KC, 128])
    w_gate_bf = dscr("w_gate_bf", [FC, 128, KC, 128])
    w_up_bf = dscr("w_up_bf", [FC, 128, KC, 128])
    w_down_bf = dscr("w_down_bf", [16, 128, FC, 128])
    QaT = dscr("QaT", [8, 128, OWN])
    KaT = dscr("KaT", [2, 128, 5120])
    Va = dscr("Va", [128, 40, 2, 129])
    QbT = dscr("QbT", [8, 128, OWN])
    KbT = dscr("KbT", [8, 128, SEQ])
    Vb = dscr("Vb", [8, 128, 64, 129])
    mixT = dscr("mixT", [16, 128, OWN])

    top = contextlib.ExitStack()
    with top:
        S = Sched(nc, top)

        def sbt(st, name, shape, dt):
            return st.enter_context(nc.sbuf_tensor(name, list(shape), dt))

        pbS = top.enter_context(nc.psum_tensor("pbS", [128, 4, 512], F32))
        banks = [pbS[:, i, :] for i in range(4)]
        banks += [top.enter_context(nc.psum_tensor("pb%d" % i, [128, 512], F32))[:, :] for i in range(4, 7)]
        pbT = top.enter_context(nc.psum_tensor("pbT", [128, 1024], BF16))
        bank_rr = [0]

        def nextbank(nb=7, base=0):
            i = base + bank_rr[0] % nb
            bank_rr[0] += 1
            return banks[i], "pb%d" % i

        ident = sbt(top, "ident", [128, 128], BF16)
        ones = sbt(top, "ones", [128, 128], BF16)
        epsb = sbt(top, "epsb", [128, 1], F32)
        mod = sbt(top, "mod", [128, 96], F32)
        G1 = sbt(top, "G1", [128, KC], F32)
        G2 = sbt(top, "G2", [128, KC], F32)
        fg_sb = sbt(top, "fg_sb", [128, KC], F32)
        neglam = sbt(top, "neglam", [128, 1], F32)
        esink = sbt(top, "esink", [128, 8], F32)
        gainB8 = sbt(top, "gainB8", [128, 128], F32)
        SH1 = mod[:, 0:16]
        g1c = mod[:, 32:48]
        SH2 = mod[:, 48:64]
        g2c = mod[:, 80:96]

        def dma(eng, out, in_, reads=(), writes=(), is_output=False):
            return S.op(eng, lambda e, o=out, i=in_: e.dma_start(out=o, in_=i), reads=reads, writes=writes,
                        dma=True, is_output=is_output)

        def mm(out, lhsT, rhs, start, stop, reads, writes, skip=False):
            if skip:
                fn = lambda e, o=out, l=lhsT, r=rhs, a=start, b=stop: e.matmul(
                    o, lhsT=l, rhs=r, start=a, stop=b, skip_group_check=True)
            else:
                fn = lambda e, o=out, l=lhsT, r=rhs, a=start, b=stop: e.matmul(o, lhsT=l, rhs=r, start=a, stop=b)
            return S.op("pe", fn, reads=reads, writes=writes)

        def act(out, in_, func, reads, writes, bias=None, scale=1.0):
            if bias is None:
                fn = lambda e, o=out, i=in_, f=func, s=scale: e.activation(out=o, in_=i, func=f, scale=s)
            else:
                fn = lambda e, o=out, i=in_, f=func, s=scale, b=bias: e.activation(out=o, in_=i, func=f, bias=b, scale=s)
            return S.op("act", fn, reads=reads, writes=writes)

        def stt(eng, out, in0, scalar, in1, op0, op1, reads, writes):
            return S.op(eng, lambda e, o=out, a=in0, s=scalar, b=in1, p=op0, q=op1: e.scalar_tensor_tensor(
                out=o, in0=a, scalar=s, in1=b, op0=p, op1=q), reads=reads, writes=writes)

        def tt(eng, out, in0, in1, op, reads, writes):
            return S.op(eng, lambda e, o=out, a=in0, b=in1, p=op: e.tensor_tensor(out=o, in0=a, in1=b, op=p),
                        reads=reads, writes=writes)

        def tcopy(eng, out, in_, reads, writes):
            if eng == "act":
                return S.op("act", lambda e, o=out, i=in_: e.copy(out=o, in_=i), reads=reads, writes=writes)
            return S.op(eng, lambda e, o=out, i=in_: e.tensor_copy(out=o, in_=i), reads=reads, writes=writes)

        def rstd_ops(out, in_, scale, reads, writes):
            act(out, in_, AF.Ln, reads=list(reads) + ["epsb"], writes=writes, bias=epsb[:, 0:1], scale=scale)
            act(out, out, AF.Exp, reads=writes, writes=writes, scale=-0.5)

        st0 = contextlib.ExitStack()
        st = st0
        if True:
            c_f = sbt(st, "c_f", [128, KC], F32)
            c_bf = sbt(st, "c_bf", [128, KC], BF16)
            bada = sbt(st, "bada", [128, 96], F32)
            n1 = sbt(st, "n1", [128, KC], F32)
            n2 = sbt(st, "n2", [128, KC], F32)
            lq = sbt(st, "lq", [128, 4, 64], F32)
            lp = sbt(st, "lp", [128, 2, 64], F32)
            ls = sbt(st, "ls", [128, 2], F32)
            sk = sbt(st, "sk", [128, 8], F32)
            gb = sbt(st, "gb", [128, 128], F32)
            wa = [sbt(st, "wa%d" % i, [128, KC, 128], BF16) for i in range(3)]

            def conv(src, dst, n, step, key):
                for c0 in range(0, n, step):
                    c1 = min(n, c0 + step)
                    dma("pool", dst[c0:c1], src[c0:c1], writes=["%s%d" % (key, c0 // step)])

            SKIP = os.environ.get("KSKIP", "")
            if "A" not in SKIP:
                conv(w_in, w_in_bf, 36, 4, "cw_in")
            dma("pool", ident[:], c_ident, writes=["ident"])
            dma("sp", c_f[:], cT, writes=["c_f"])
            dma("sp", bada[:], b_ada, writes=["bada"])
            dma("sp", n1[:], n1g, writes=["n1"])
            dma("sp", n2[:], n2g, writes=["n2"])
            dma("sp", fg_sb[:], fgn, writes=["fg"])
            dma("sp", lq[:], lqk, writes=["lq"])
            dma("sp", sk[:], sinkb, writes=["sk"])
            dma("sp", gb[:], gainB, writes=["gb"])
            S.op("dve", lambda e: e.memset(ones[:], 1.0), writes=["ones"])
            S.op("dve", lambda e: e.memset(epsb[:], EPS), writes=["epsb"])
            act(c_bf[:], c_f[:], AF.Silu, reads=["c_f"], writes=["c_bf"])
            pm = banks[0]

            def ada_load(ec):
                dma("pool", wa[ec % 3][:], w_ada[ec], writes=["wa%d" % (ec % 3)])

            def ada_mm(ec):
                w = wa[ec % 3]
                wk = "wa%d" % (ec % 3)
                for kc in range(KC):
                    mm(pm[:, ec:ec + 1], w[:, kc, :], c_bf[:, kc:kc + 1], kc == 0, kc == KC - 1,
                       reads=[wk, "c_bf"], writes=["pb0"])

            def ada_chunk(ec):
                ada_load(ec)
                ada_mm(ec)

            for ec in range(32):
                ada_chunk(ec)
            tt("dve", mod[:, 0:32], pm[:, 0:32], bada[:, 0:32], ALU.add, reads=["pb0", "bada"], writes=["mod"])
            stt("dve", G1[:], mod[:, 16:32], 1.0, n1[:], ALU.add, ALU.mult, reads=["mod", "n1"], writes=["G1"])
            slots = [[] for _ in range(64 + 2)]
            for k_, ec in enumerate(range(32, 96)):
                slots[k_].append(lambda ec=ec: ada_load(ec))
                slots[k_ + 2].append(lambda ec=ec: ada_mm(ec))
            ada_q = [(lambda sl=sl: [f_() for f_ in sl]) for sl in slots]

            def ada_fin():
                tt("dve", mod[:, 32:96], pm[:, 32:96], bada[:, 32:96], ALU.add, reads=["pb0", "bada"], writes=["mod"])
                stt("dve", G2[:], mod[:, 64:80], 1.0, n2[:], ALU.add, ALU.mult, reads=["mod", "n2"], writes=["G2"])
            ada_q.append(ada_fin)
            tt("dve", lp[:, 0, :], lq[:, 0, :], lq[:, 1, :], ALU.mult, reads=["lq"], writes=["lp"])
            tt("dve", lp[:, 1, :], lq[:, 2, :], lq[:, 3, :], ALU.mult, reads=["lq"], writes=["lp"])
            S.op("dve", lambda e: e.reduce_sum(out=ls[:], in_=lp[:], axis=AX.X), reads=["lp"], writes=["ls"])
            act(ls[:], ls[:], AF.Exp, reads=["ls"], writes=["ls"])
            tt("dve", neglam[:], ls[:, 1:2], ls[:, 0:1], ALU.subtract, reads=["ls"], writes=["neglam"])
            S.op("dve", lambda e: e.tensor_scalar_add(out=neglam[:], in0=neglam[:], scalar1=-LAM_INIT),
                 reads=["neglam"], writes=["neglam"])
            act(esink[:], sk[:], AF.Exp, reads=["sk"], writes=["esink"])
            S.op("dve", lambda e: e.tensor_scalar_mul(out=gainB8[:], in0=gb[:], scalar1=1.0 - LAM_INIT),
                 reads=["gb"], writes=["gainB8"])
            if debug_outputs:
                dbg = nc.dram_tensor("dbg", [128, 256], F32, kind="ExternalOutput").ap()
                dma("sp", dbg[:, 0:96], mod[:], reads=["mod"])
                dma("sp", dbg[:, 96:112], G1[:], reads=["G1"])
                dma("sp", dbg[:, 112:128], G2[:], reads=["G2"])
                dma("sp", dbg[:, 129:137], esink[:], reads=["esink"])
                dma("sp", dbg[:, 137:139], ls[:], reads=["ls"])
            p1_on = stop_after >= 1 and "1" not in os.environ.get("KNOPH", "")
            if not p1_on:
                while ada_q:
                    ada_q.pop(0)()
                S.emit()
                st0.close()

        if stop_after >= 1 and "1" not in os.environ.get("KNOPH", ""):
            with contextlib.ExitStack() as st:
                hT = [sbt(st, "hT%d" % i, [128, KC, 1024], BF16) for i in range(2)]
                xs = [sbt(st, "xs%d" % i, [128, KC, 256], F32) for i in range(2)]
                sq = [sbt(st, "sq%d" % i, [128, KC, 256], BF16) for i in range(2)]
                rs = [sbt(st, "rs%d" % i, [128, 256], F32) for i in range(2)]
                htmp = [sbt(st, "htmp%d" % i, [128, 256], F32) for i in range(2)]
                wb = [sbt(st, "wb%d" % i, [128, KC, 128], BF16) for i in range(3)]
                wv = [sbt(st, "wv%d" % i, [128, 4, KC, 128], BF16) for i in range(2)]
                stg = [sbt(st, "stg%d" % i, [128, 512], BF16) for i in range(4)]
                vst = [sbt(st, "vst%d" % i, [128, 4, 129], BF16) for i in range(4)]
                for i in range(4):
                    S.op("dve", lambda e, t=vst[i]: e.memset(t[:], 1.0), writes=["vst%d" % i])
                cnt = {"sg": 0, "w": 0, "wv": 0, "stg": 0, "vst": 0, "ev": 0}

                def subgroup(e8, sgi):
                    hb = e8 % 2
                    t0 = e8 * 1024 + sgi * 256
                    i = cnt["sg"] % 2
                    cnt["sg"] += 1
                    x_, q_, r_ = xs[i], sq[i], rs[i]
                    xk, qk, rk = "xs%d" % i, "sq%d" % i, "rs%d" % i
                    dma("sp", x_[:], xT[:, :, t0:t0 + 256], writes=[xk])
                    act(q_[:], x_[:], AF.Square, reads=[xk], writes=[qk])
                    bk, bkey = nextbank(6, 1)
                    for kc in range(KC):
                        mm(bk[:, 0:256], ones[:], q_[:, kc, :], kc == 0, kc == KC - 1, reads=[qk, "ones"], writes=[bkey])
                    rstd_ops(r_[:], bk[:, 0:256], 1.0 / D, reads=[bkey], writes=[rk])
                    hk = "hT%d_%d" % (hb, sgi // 2)
                    for kc in range(KC):
                        tm = htmp[kc % 2]
                        tk = "htmp%d" % (kc % 2)
                        stt("dve", tm[:], x_[:, kc, :], G1[:, kc:kc + 1], r_[:], ALU.mult, ALU.mult,
                            reads=[xk, rk, "G1"], writes=[tk])
                        act(hT[hb][:, kc, sgi * 256:(sgi + 1) * 256], tm[:], AF.Identity, reads=[tk, "mod"],
                            writes=[hk], bias=SH1[:, kc:kc + 1])

                def evac(out, in_, reads, writes):
                    eng = "act" if cnt["ev"] % 2 == 0 else "dve"
                    cnt["ev"] += 1
                    tcopy(eng, out, in_, reads, writes)

                def fm_chunk(e8, cc, dst, tgs):
                    hb = e8 % 2
                    i = cnt["w"] % 3
                    cnt["w"] += 1
                    w, wk = wb[i], "wb%d" % i
                    dma("sp", w[:], w_in_bf[cc], reads=["cw_in%d" % (cc // 4)], writes=[wk])
                    bl = [nextbank(6, 1) for _ in tgs]
                    for kc in range(KC):
                        for j, tg in enumerate(tgs):
                            mm(bl[j][0][:, :], w[:, kc, :], hT[hb][:, kc, tg * 512:(tg + 1) * 512], kc == 0, kc == KC - 1,
                               reads=[wk, "hT%d_%d" % (hb, tg)], writes=[bl[j][1]])
                    for j, tg in enumerate(tgs):
                        si = cnt["stg"] % 4
                        cnt["stg"] += 1
                        evac(stg[si][:], bl[j][0][:, :], reads=[bl[j][1]], writes=["stg%d" % si])
                        t0 = e8 * 1024 + tg * 512
                        dma("pool", dst[:, t0:t0 + 512], stg[si][:], reads=["stg%d" % si])

                def tm_chunks(e8, c0, ncol, dstfn, tbs):
                    hb = e8 % 2
                    i = cnt["wv"] % 2
                    cnt["wv"] += 1
                    w, wk = wv[i], "wv%d" % i
                    dma("sp", w[:, 0:ncol, :, :], w_in_bf[c0:c0 + ncol].rearrange("c p k j -> p c k j"),
                        reads=["cw_in%d" % (c0 // 4)], writes=[wk])
                    for tb in tbs:
                        bk, bkey = nextbank(6, 1)
                        o = bk[:, 0:ncol * 128].rearrange("p (c j) -> p c j", j=128)
                        for kc in range(KC):
                            mm(o, hT[hb][:, kc, tb * 128:(tb + 1) * 128], w[:, 0:ncol, kc, :], kc == 0, kc == KC - 1,
                               reads=[wk, "hT%d_%d" % (hb, tb // 4)], writes=[bkey])
                        vi = cnt["vst"] % 4
                        cnt["vst"] += 1
                        evac(vst[vi][:, 0:ncol, 0:128], o, reads=[bkey], writes=["vst%d" % vi])
                        kb = e8 * 8 + tb
                        dma("pool", dstfn(kb), vst[vi][:, 0:ncol, :], reads=["vst%d" % vi])

                for sgi in range(4):
                    subgroup(0, sgi)
                for e8 in range(8):
                    own = e8 < 4
                    work = []
                    if own:
                        for cc in range(8):
                            work.append(("fm", cc, QaT[cc], (0, 1)))
                        for cc in (8, 9):
                            work.append(("fm", cc, KaT[cc - 8], (0, 1)))
                        work.append(("tm", 10, 2, (lambda kb: Va[:, kb, :, :]), tuple(range(8))))
                        for cc in range(12, 20):
                            work.append(("fm", cc, QbT[cc - 12], (0, 1)))
                    elif e8 == 4:
                        for cc in (8, 9):
                            work.append(("fm", cc, KaT[cc - 8], (0,)))
                        work.append(("tm", 10, 2, (lambda kb: Va[:, kb, :, :]), (0,)))
                    for cc in range(20, 28):
                        work.append(("fm", cc, KbT[cc - 20], (0, 1)))
                    for hf in range(2):
                        work.append(("tm", 28 + 4 * hf, 4,
                                     (lambda kb, hf=hf: Vb[4 * hf:4 * hf + 4, :, kb, :].rearrange("h p e -> p h e")),
                                     tuple(range(8))))
                    nxt = 0
                    for wi, item in enumerate(work):
                        if item[0] == "fm":
                            fm_chunk(e8, item[1], item[2], item[3])
                        else:
                            tm_chunks(e8, item[1], item[2], item[3], item[4])
                        if e8 + 1 < 8 and nxt < 4 and wi % 2 == 1:
                            subgroup(e8 + 1, nxt)
                            nxt += 1
                        if ada_q:
                            ada_q.pop(0)()
                    while e8 + 1 < 8 and nxt < 4:
                        subgroup(e8 + 1, nxt)
                        nxt += 1
                while ada_q:
                    ada_q.pop(0)()
                S.emit()
            st0.close()

        if stop_after >= 2:
            with contextlib.ExitStack() as st:
                qa = sbt(st, "qa", [128, 8, OWN], BF16)
                ka = sbt(st, "ka", [128, 2, 4224], BF16)
                va = sbt(st, "va", [128, 33, 2, 129], BF16)
                bA = sbt(st, "bA", [128, 6, 512], F32)
                gA = sbt(st, "gA", [128, 1024], F32)
                tmpA = [sbt(st, "tmpA%d" % i, [128, 512], F32) for i in range(2)]
                ptA = [sbt(st, "ptA%d" % i, [128, 512], BF16) for i in range(2)]
                oa = sbt(st, "oa", [128, 1024], F32)
                oq = sbt(st, "oq", [128, 1024], F32)
                oab = sbt(st, "oab", [128, 1024], BF16)
                zz = sbt(st, "zz", [128, 8], F32)
                ssA = sbt(st, "ssA", [128, 1], F32)
                mst = [sbt(st, "mst%d" % i, [128, 8, 512], BF16) for i in range(2)]
                for h_ in range(8):
                    dma("sp", qa[:, h_, :], QaT[h_], writes=["qa"])
                dma("sp", ka[:], KaT[:, :, 0:4224].rearrange("g p t -> p g t"), writes=["ka"])
                dma("sp", va[:], Va[:, 0:33, :, :], writes=["va"])
                dma("sp", bA[:], c_biasA, writes=["bA"])
                dma("sp", gA[:], gainA, writes=["gA"])
                def accA(h):
                    return banks[4 + h // 3][:, (h % 3) * 129:(h % 3) * 129 + 129], "pb%d" % (4 + h // 3)
                accsb = [sbt(st, "accsb%d" % i, [128, 8, 129], F32) for i in range(2)]
                NB2 = int(os.environ.get("KN2", "32"))
                steps = []
                for n in range(NB2):
                    for g in range(2):
                        ts = [t for t in range(3) if 0 <= n - 1 + t <= 32]
                        for ti, t in enumerate(ts):
                            steps.append((n, g, t, ti == 0, ti == len(ts) - 1, g == 1 and ti == len(ts) - 1))
                import collections
                deferA = collections.deque()

                def emit_sA(step):
                    n, g, t, first, last, blk_end = step
                    kb = n - 1 + t
                    bk, bkey = nextbank(4)
                    o = bk[:, :].rearrange("p (h q) -> p h q", q=128)
                    mm(o, ka[:, g, kb * 128:(kb + 1) * 128], qa[:, 4 * g:4 * g + 4, n * 128:(n + 1) * 128], True, True,
                       reads=["ka", "qa"], writes=[bkey])
                    return bk, bkey

                def chain(n, asb, ask):
                    ops = []
                    ops.append(lambda: tt("dve", zz[:], asb[:, :, 128], esink[:], ALU.add, reads=[ask, "esink"], writes=["zz"]))
                    ops.append(lambda: S.op("dve", lambda e: e.reciprocal(out=zz[:], in_=zz[:]), reads=["zz"], writes=["zz"]))
                    for h in range(8):
                        ops.append(lambda h=h: S.op(
                            "dve", lambda e, o=oa[:, h * 128:(h + 1) * 128], a=asb[:, h, 0:128], s_=zz[:, h:h + 1]:
                            e.tensor_scalar_mul(out=o, in0=a, scalar1=s_), reads=[ask, "zz"], writes=["oa"]))
                    ops.append(lambda: tt("dve", oq[:], oa[:], oa[:], ALU.mult, reads=["oa"], writes=["oq"]))
                    ops.append(lambda: S.op("dve", lambda e: e.reduce_sum(out=ssA[:], in_=oq[:], axis=AX.X),
                                            reads=["oq"], writes=["ssA"]))
                    ops.append(lambda: act(ssA[:], ssA[:], AF.Ln, reads=["ssA", "epsb"], writes=["ssA"],
                                           bias=epsb[:, 0:1], scale=1.0 / 1024.0))
                    ops.append(lambda: act(ssA[:], ssA[:], AF.Exp, reads=["ssA"], writes=["ssA"], scale=-0.5))
                    ops.append(lambda: stt("dve", oab[:], oa[:], ssA[:, 0:1], gA[:], ALU.mult, ALU.mult,
                                           reads=["oa", "ssA", "gA"], writes=["oab"]))
                    ms = mst[(n // 4) % 2]
                    mk = "mst%d" % ((n // 4) % 2)

                    def tr_all():
                        for h in range(8):
                            pslot = pbT[:, h * 128:h * 128 + 128]
                            S.op("pe", lambda e, o=pslot, i_=oab[:, h * 128:(h + 1) * 128]: e.transpose(o, i_, ident[:]),
                                 reads=["oab", "ident"], writes=["pbT"])
                    ops.append(tr_all)
                    ops.append(lambda: tcopy("dve", ms[:, :, (n % 4) * 128:(n % 4) * 128 + 128],
                                             pbT[:, :].rearrange("p (h q) -> p h q", q=128), reads=["pbT"], writes=[mk]))
                    if n % 4 == 3:
                        t0 = (n // 4) * 512
                        ops.append(lambda: dma("pool", mixT[0:8, :, t0:t0 + 512].rearrange("c p t -> p c t"), ms[:], reads=[mk]))
                    return ops

                sq_ = [emit_sA(st_) for st_ in steps[:2]]
                for idx, step in enumerate(steps):
                    n, g, t, first, last, blk_end = step
                    kb = n - 1 + t
                    bk, bkey = sq_.pop(0)
                    i = idx % 2
                    stt("dve", tmpA[i][:], bk[:, :], SCALE_A, bA[:, g * 3 + t, :], ALU.mult, ALU.add,
                        reads=[bkey, "bA"], writes=["tmpA%d" % i])
                    act(ptA[i][:], tmpA[i][:], AF.Exp, reads=["tmpA%d" % i], writes=["ptA%d" % i])
                    if idx + 2 < len(steps):
                        sq_.append(emit_sA(steps[idx + 2]))
                    for hh in range(4):
                        h = 4 * g + hh
                        ao, ak = accA(h)
                        mm(ao, ptA[i][:, hh * 128:(hh + 1) * 128], va[:, kb, g, :], first and h in (0, 3, 6),
                           last, reads=["ptA%d" % i, "va"], writes=[ak], skip=True)
                    if blk_end:
                        asb, ask = accsb[n % 2], "accsb%d" % (n % 2)
                        for b_ in range(3):
                            nh = 3 if b_ < 2 else 2
                            tcopy("dve", asb[:, 3 * b_:3 * b_ + nh, :],
                                  banks[4 + b_][:, 0:nh * 129].rearrange("p (h e) -> p h e", e=129),
                                  reads=["pb%d" % (4 + b_)], writes=[ask])
                        deferA.extend(chain(n, asb, ask))
                    for _ in range(4):
                        if deferA:
                            deferA.popleft()()
                while deferA:
                    deferA.popleft()()
                S.emit()

        if stop_after >= 3:
            with contextlib.ExitStack() as st:
                kbt = [sbt(st, "kbt%d" % i, [128, SEQ], BF16) for i in range(2)]
                vbt = [sbt(st, "vbt%d" % i, [128, 64, 129], BF16) for i in range(2)]
                qbt = [sbt(st, "qbt%d" % i, [128, OWN], BF16) for i in range(2)]
                td = sbt(st, "td", [128, 896], F32)
                bB = sbt(st, "bB", [128, 8, 126], F32)
                ff = sbt(st, "ff", [128, 8, 8], F32)
                tmpD = [sbt(st, "tmpD%d" % i, [128, 2, 512], F32) for i in range(3)]
                dcnt = [0]
                pt = [sbt(st, "pt%d" % i, [128, 2, 512], BF16) for i in range(4)]
                accS_sets = [[sbt(st, "accS%d_%d" % (k_, i), [128, 129], F32) for i in range(8)] for k_ in range(2)]
                accS = accS_sets[0]
                cset = [0]
                import collections
                defer = collections.deque()
                rr = sbt(st, "rr", [128, 2], F32)
                uu = sbt(st, "uu", [128, 1], F32)
                od = sbt(st, "od", [128, 128], F32)
                oq2 = sbt(st, "oq2", [128, 128], F32)
                ssB = sbt(st, "ssB", [128, 1], F32)
                ob = sbt(st, "ob", [128, 128], BF16)
                obT = [sbt(st, "obT%d" % i, [128, 512], BF16) for i in range(2)]
                dma("sp", td[:], c_tdist, writes=["td"])
                dma("sp", bB[:], c_biasB, writes=["bB"])
                dma("sp", ff[:], c_fbfa, writes=["ff"])

                def accB(j):
                    return banks[4 + j // 3][:, (j % 3) * 129:(j % 3) * 129 + 129], "pb%d" % (4 + j // 3)

                later_conv = []
                for src_, dst_, n_, step_ in ((w_o, w_o_bf, 16, 4), (w_gate, w_gate_bf, FC, 4), (w_up, w_up_bf, FC, 4),
                                              (w_down, w_down_bf, 16, 1)):
                    for c0 in range(0, n_, step_):
                        c1 = min(n_, c0 + step_)
                        later_conv.append((src_[c0:c1], dst_[c0:c1]))
                junk = pbT.bitcast(F32)[:, 256:512]
                NJUNK = int(os.environ.get("KJUNK", "0"))
                sbank = [0]

                def next_s():
                    i = sbank[0] % 4
                    sbank[0] += 1
                    return banks[i], "pb%d" % i

                pti = [0]
                fin = [0]
                for h in range(int(os.environ.get("KH3", "8"))):
                    hb = h % 2
                    kk, vk, qk = "kbt%d" % hb, "vbt%d" % hb, "qbt%d" % hb
                    dma("sp", kbt[hb][:], KbT[h], writes=[kk])
                    dma("sp", vbt[hb][:], Vb[h], writes=[vk])
                    dma("sp", qbt[hb][:], QbT[h], writes=[qk])
                    K_, V_, Q_ = kbt[hb], vbt[hb], qbt[hb]
                    slope = SLOPES_B[h]
                    items = []
                    for qg in range(int(os.environ.get("KQG3", "8"))):
                        diag = list(range(4 * qg, 4 * qg + 4))
                        before = list(range(0, 4 * qg))
                        after = list(range(4 * qg + 4, 64))
                        for ph, lst in (("d", diag), ("b", before), ("a", after)):
                            for ii, kb in enumerate(lst):
                                items.append((qg, ph, kb, ii == 0, ii == len(lst) - 1))

                    def emit_scores(item):
                        qg, ph, kb, first, last = item
                        sp_ = sbank[0] % 2
                        sbank[0] += 1
                        for c in range(2):
                            mm(pbS[:, 2 * sp_ + c, :], K_[64 * c:64 * c + 64, kb * 128:(kb + 1) * 128],
                               Q_[64 * c:64 * c + 64, qg * 512:(qg + 1) * 512], True, True,
                               reads=[kk, qk], writes=["pb%d" % (2 * sp_ + c)])
                        di = None
                        if ph == "d":
                            di = dcnt[0] % 3
                            dcnt[0] += 1
                            o_ = kb - 4 * qg
                            for c in range(2):
                                stt("dve", tmpD[di][:, c, :], td[:, 384 - 128 * o_:896 - 128 * o_], -slope / SCALE_B,
                                    pbS[:, 2 * sp_ + c, :], ALU.mult, ALU.add, reads=["td", "pb%d" % (2 * sp_ + c)],
                                    writes=["tmpD%d" % di])
                        return sp_, di

                    sq_ = [emit_scores(items[0]), emit_scores(items[1])]
                    for idx, item in enumerate(items):
                        qg, ph, kb, first, last = item
                        cur, di = sq_.pop(0)
                        pi = pti[0] % 4
                        pti[0] += 1
                        P_, pk = pt[pi], "pt%d" % pi
                        skeys = ["pb%d" % (2 * cur), "pb%d" % (2 * cur + 1)]
                        spair = pbS[:, 2 * cur:2 * cur + 2, :]
                        if ph == "d":
                            act(P_[:], tmpD[di][:], AF.Exp, reads=["tmpD%d" % di] + skeys, writes=[pk], scale=SCALE_B)
                        elif ph == "b":
                            d_ = 4 * qg - kb
                            act(P_[:], spair, AF.Exp, reads=skeys + ["bB"], writes=[pk],
                                bias=bB[:, h, d_ - 1:d_], scale=SCALE_B)
                        else:
                            d_ = kb - (4 * qg + 4)
                            act(P_[:], spair, AF.Exp, reads=skeys + ["bB"], writes=[pk],
                                bias=bB[:, h, 63 + d_:64 + d_], scale=SCALE_B)
                        if idx + 2 < len(items):
                            sq_.append(emit_scores(items[idx + 2]))
                        for c in range(2):
                            for a in range(4):
                                ao, ak = accB(2 * a + c)
                                mm(ao, P_[:, c, a * 128:(a + 1) * 128], V_[:, kb, :], first and (2 * a + c) in (0, 4, 6), last,
                                   reads=[pk, vk], writes=[ak], skip=True)
                        for _ in range(NJUNK):
                            mm(junk, ones[:], K_[:, kb * 128:kb * 128 + 256] if kb < 63 else K_[:, 0:256], True, True,
                               reads=["ones", kk], writes=["pbT"], skip=True)
                        if last:
                            for a in range(4):
                                for c in range(2):
                                    j = 2 * a + c
                                    ao, ak = accB(j)
                                    sk2 = "accSs%d_%d" % (cset[0], j)
                                    if ph == "d":
                                        tcopy("dve", accS[j][:], ao, reads=[ak], writes=[sk2])
                                    else:
                                        col = 2 * a + (0 if ph == "b" else 1)
                                        stt("dve", accS[j][:], ao, ff[:, h, col:col + 1], accS[j][:], ALU.mult, ALU.add,
                                            reads=[ak, "ff", sk2], writes=[sk2])
                            lastphase = (ph == "a")
                            if lastphase:
                                oi = fin[0] % 2
                                fin[0] += 1
                                OT, otk = obT[oi], "obT%d" % oi
                                aset = accS
                                sk_ = "s%d_" % cset[0]
                                for a in range(4):
                                    a0, a1 = aset[2 * a], aset[2 * a + 1]
                                    k0, k1 = "accS%s%d" % (sk_, 2 * a), "accS%s%d" % (sk_, 2 * a + 1)
                                    defer.append(lambda x=a0, k0=k0: S.op(
                                        "dve", lambda e, x=x: e.reciprocal(out=rr[:, 0:1], in_=x[:, 128:129]),
                                        reads=[k0], writes=["rr"]))
                                    defer.append(lambda x=a1, k1=k1: S.op(
                                        "dve", lambda e, x=x: e.reciprocal(out=rr[:, 1:2], in_=x[:, 128:129]),
                                        reads=[k1], writes=["rr"]))
                                    defer.append(lambda: tt("dve", uu[:], rr[:, 1:2], neglam[:], ALU.mult,
                                                            reads=["rr", "neglam"], writes=["uu"]))
                                    defer.append(lambda x=a0, k0=k0: S.op(
                                        "dve", lambda e, x=x: e.tensor_scalar_mul(out=od[:], in0=x[:, 0:128], scalar1=rr[:, 0:1]),
                                        reads=[k0, "rr"], writes=["od"]))
                                    defer.append(lambda x=a1, k1=k1: stt("dve", od[:], x[:, 0:128], uu[:, 0:1], od[:], ALU.mult, ALU.add,
                                                                         reads=[k1, "uu", "od"], writes=["od"]))
                                    defer.append(lambda: tt("dve", oq2[:], od[:], od[:], ALU.mult, reads=["od"], writes=["oq2"]))
                                    defer.append(lambda: S.op("dve", lambda e: e.reduce_sum(out=ssB[:], in_=oq2[:], axis=AX.X),
                                                              reads=["oq2"], writes=["ssB"]))
                                    defer.append(lambda: act(ssB[:], ssB[:], AF.Ln, reads=["ssB", "epsb"], writes=["ssB"],
                                                             bias=epsb[:, 0:1], scale=1.0 / 128.0))
                                    defer.append(lambda: act(ssB[:], ssB[:], AF.Exp, reads=["ssB"], writes=["ssB"], scale=-0.5))
                                    defer.append(lambda: stt("dve", ob[:], od[:], ssB[:, 0:1], gainB8[:], ALU.mult, ALU.mult,
                                                             reads=["od", "ssB", "gainB8"], writes=["ob"]))
                                    pslot = pbT[:, a * 128:(a + 1) * 128]
                                    defer.append(lambda o=pslot: S.op("pe", lambda e, o=o: e.transpose(o, ob[:], ident[:]),
                                                                      reads=["ob", "ident"], writes=["pbT"]))
                                    defer.append(lambda o=pslot, OT=OT, otk=otk, a=a: tcopy(
                                        "dve", OT[:, a * 128:(a + 1) * 128], o, reads=["pbT"], writes=[otk]))

                                def _store(OT=OT, otk=otk, h=h, qg=qg):
                                    dma("pool", mixT[8 + h][:, qg * 512:(qg + 1) * 512], OT[:], reads=[otk])
                                    if later_conv:
                                        src_, dst_ = later_conv.pop(0)
                                        dma("pool", dst_, src_)
                                defer.append(_store)
                                cset[0] ^= 1
                                accS = accS_sets[cset[0]]
                        if defer:
                            defer.popleft()()
                while defer:
                    defer.popleft()()
                while later_conv:
                    src_, dst_ = later_conv.pop(0)
                    dma("pool", dst_, src_)
                S.emit()

        if stop_after >= 4:
            with contextlib.ExitStack() as st:
                xts = [sbt(st, "xt%d" % i, [128, KC, 512], F32) for i in range(2)]
                mx = sbt(st, "mx", [128, KC, 512], BF16)
                h2 = sbt(st, "h2", [128, KC, 512], BF16)
                ffT = sbt(st, "ffT", [128, FC, 512], BF16)
                w16 = [sbt(st, "w16_%d" % i, [128, KC, 128], BF16) for i in range(4)]
                wd = [sbt(st, "wd%d" % i, [128, FC, 128], BF16) for i in range(2)]
                sg = [sbt(st, "sg%d" % i, [128, 512], F32) for i in range(2)]
                rsA = sbt(st, "rsA", [128, 512], F32)
                rsB = sbt(st, "rsB", [128, 512], F32)
                ht4 = [sbt(st, "ht4_%d" % i, [128, 512], F32) for i in range(2)]
                sqs = [sbt(st, "sqs%d" % i, [128, 512], BF16) for i in range(3)]
                wcnt = [0]
                sqc = [0]
                NTG = int(os.environ.get("KTG4", "8"))
                SSA, SSB = banks[6], banks[5]

                def loadw(src, key):
                    i = wcnt[0] % 4
                    wcnt[0] += 1
                    dma("sp", w16[i][:], src, reads=[key], writes=["w16_%d" % i])
                    return w16[i], "w16_%d" % i

                def sq_and_ss(xt_, xk, dc, ssb, sskey, pend):
                    i = sqc[0] % 3
                    sqc[0] += 1
                    act(sqs[i][:], xt_[:, dc, :], AF.Square, reads=[xk], writes=["sqs%d" % i])
                    pend.append(lambda i=i, dc=dc: mm(ssb[:, :], ones[:], sqs[i][:], dc == 0, dc == KC - 1,
                                                      reads=["sqs%d" % i, "ones"], writes=[sskey]))

                def WO(tg):
                    t0 = tg * 512
                    xt_, xk = xts[tg % 2], "xt%d" % (tg % 2)
                    dma("sp", xt_[:], xT[:, :, t0:t0 + 512], writes=[xk])
                    dma("sp", mx[:], mixT[:, :, t0:t0 + 512].rearrange("c p t -> p c t"), writes=["mx"])
                    pend = []
                    for dc in range(16):
                        w, wk = loadw(w_o_bf[dc], "cw_o%d" % (dc // 4))
                        bk, bkey = nextbank(5)
                        for kc in range(KC):
                            mm(bk[:, :], w[:, kc, :], mx[:, kc, :], kc == 0, kc == KC - 1, reads=[wk, "mx"], writes=[bkey])
                        stt("dve", xt_[:, dc, :], bk[:, :], g1c[:, dc:dc + 1], xt_[:, dc, :], ALU.mult, ALU.add,
                            reads=[bkey, "mod", xk], writes=[xk])
                        sq_and_ss(xt_, xk, dc, SSA, "pb6", pend)
                        if len(pend) > 2:
                            pend.pop(0)()
                    return pend

                def NORM2_ops(tg, pend):
                    xt_, xk = xts[tg % 2], "xt%d" % (tg % 2)
                    ops = []
                    for p_ in pend:
                        ops.append(p_)
                    ops.append(lambda: rstd_ops(rsA[:], SSA[:, :], 1.0 / D, reads=["pb6"], writes=["rsA"]))
                    for kc in range(KC):
                        def f(kc=kc):
                            tm, tk = ht4[kc % 2], "ht4_%d" % (kc % 2)
                            stt("dve", tm[:], xt_[:, kc, :], G2[:, kc:kc + 1], rsA[:], ALU.mult, ALU.mult,
                                reads=[xk, "rsA", "G2"], writes=[tk])
                            act(h2[:, kc, :], tm[:], AF.Identity, reads=[tk, "mod"], writes=["h2"], bias=SH2[:, kc:kc + 1])
                        ops.append(f)
                    return ops

                def FINAL_ops(tg, pend):
                    t0 = tg * 512
                    xt_, xk = xts[tg % 2], "xt%d" % (tg % 2)
                    ops = list(pend)
                    ops.append(lambda: rstd_ops(rsB[:], SSB[:, :], 1.0 / D, reads=["pb5"], writes=["rsB"]))
                    for kc in range(KC):
                        ops.append(lambda kc=kc: stt("dve", xt_[:, kc, :], xt_[:, kc, :], fg_sb[:, kc:kc + 1], rsB[:], ALU.mult, ALU.mult,
                                                     reads=[xk, "rsB", "fg"], writes=[xk]))
                    ops.append(lambda: dma("pool", outT[:, :, t0:t0 + 512], xt_[:], reads=[xk], is_output=True))
                    return ops

                pend = WO(0)
                for f_ in NORM2_ops(0, pend):
                    f_()
                fin_ops = []
                for tg in range(NTG):
                    xt_, xk = xts[tg % 2], "xt%d" % (tg % 2)
                    for fc in range(FC):
                        wg, wgk = loadw(w_gate_bf[fc], "cw_gate%d" % (fc // 4))
                        wu, wuk = loadw(w_up_bf[fc], "cw_up%d" % (fc // 4))
                        bg, bgk = nextbank(5)
                        bu, buk = nextbank(5)
                        for kc in range(KC):
                            mm(bg[:, :], wg[:, kc, :], h2[:, kc, :], kc == 0, kc == KC - 1, reads=[wgk, "h2"], writes=[bgk])
                        for kc in range(KC):
                            mm(bu[:, :], wu[:, kc, :], h2[:, kc, :], kc == 0, kc == KC - 1, reads=[wuk, "h2"], writes=[buk])
                        s_, sk_ = sg[fc % 2], "sg%d" % (fc % 2)
                        act(s_[:], bg[:, :], AF.Silu, reads=[bgk], writes=[sk_])
                        tt("dve", ffT[:, fc, :], s_[:], bu[:, :], ALU.mult, reads=[sk_, buk], writes=["ffT"])
                        if fin_ops:
                            fin_ops.pop(0)()
                    while fin_ops:
                        fin_ops.pop(0)()
                    n2_ops = []
                    if tg + 1 < NTG:
                        n2_ops = NORM2_ops(tg + 1, WO(tg + 1))
                    pend = []
                    for dc in range(16):
                        i = dc % 2
                        dma("sp", wd[i][:], w_down_bf[dc], reads=["cw_down%d" % dc], writes=["wd%d" % i])
                        bk, bkey = nextbank(5)
                        for fc in range(FC):
                            mm(bk[:, :], wd[i][:, fc, :], ffT[:, fc, :], fc == 0, fc == FC - 1,
                               reads=["wd%d" % i, "ffT"], writes=[bkey])
                        stt("dve", xt_[:, dc, :], bk[:, :], g2c[:, dc:dc + 1], xt_[:, dc, :], ALU.mult, ALU.add,
                            reads=[bkey, "mod", xk], writes=[xk])
                        sq_and_ss(xt_, xk, dc, SSB, "pb5", pend)
                        if len(pend) > 2:
                            pend.pop(0)()
                        for _ in range(2):
                            if n2_ops:
                                n2_ops.pop(0)()
                    while n2_ops:
                        n2_ops.pop(0)()
                    fin_ops = FINAL_ops(tg, pend)
                while fin_ops:
                    fin_ops.pop(0)()
                S.emit()
    return nc


def _consts():
    p = np.arange(128, dtype=np.float64)[:, None]
    f = np.arange(128, dtype=np.float64)[None, :]
    biasA = np.zeros((128, 6, 512), dtype=np.float32)
    for g in range(2):
        for t in range(3):
            for hh in range(4):
                sl = SLOPES_A[4 * g + hh]
                if t == 0:
                    dist = 128 + f - p
                    valid = f <= p
                elif t == 1:
                    dist = np.abs(f - p)
                    valid = np.ones_like(dist, dtype=bool)
                else:
                    dist = 128 + p - f
                    valid = p <= f
                b = np.where(valid, -sl * dist, NEG)
                biasA[:, g * 3 + t, hh * 128:(hh + 1) * 128] = b.astype(np.float32)
    gg = np.arange(896, dtype=np.float64)[None, :]
    tdist = np.abs(gg - 384 - p).astype(np.float32)
    biasB = np.zeros((128, 8, 126), dtype=np.float32)
    fbfa = np.zeros((128, 8, 8), dtype=np.float32)
    pp = np.arange(128, dtype=np.float64)
    for h in range(8):
        sl = SLOPES_B[h]
        for d in range(1, 64):
            biasB[:, h, d - 1] = -sl * (128 * d - pp)
        for d in range(0, 63):
            biasB[:, h, 63 + d] = -sl * (128 * d + pp + 1)
        for a in range(4):
            fbfa[:, h, 2 * a] = np.exp(-sl * (128 * a + pp))
            fbfa[:, h, 2 * a + 1] = np.exp(-sl * (511 - 128 * a - pp))
    return {
        "c_ident": np.eye(128, dtype=np.float32),
        "c_biasA": biasA,
        "c_tdist": tdist,
        "c_biasB": biasB,
        "c_fbfa": fbfa,
    }


def _chunk_w(w, kc, nc_):
    return np.ascontiguousarray(w.reshape(kc, 128, nc_, 128).transpose(2, 1, 0, 3))


def _vec16(v):
    return np.ascontiguousarray(v.reshape(KC, 128).T)


def make_in_maps(x, c, w_ada, b_ada, norm1_gain, w_in, a_sink, a_out_gain, diff_lq1, diff_lk1, diff_lq2, diff_lk2,
                 diff_subln_gain, w_o, norm2_gain, w_gate, w_up, w_down, final_gain):
    f = lambda a: np.asarray(a, dtype=np.float32)
    x = f(x)
    shared = dict(_consts())
    shared["w_ada"] = _chunk_w(f(w_ada)[0], KC, 96)
    shared["b_ada"] = np.ascontiguousarray(f(b_ada)[0].reshape(96, 128).T)
    shared["n1g"] = _vec16(f(norm1_gain)[0])
    shared["n2g"] = _vec16(f(norm2_gain)[0])
    shared["fgn"] = _vec16(f(final_gain))
    shared["w_in"] = _chunk_w(f(w_in)[0], KC, 36)
    shared["w_o"] = _chunk_w(f(w_o)[0], KC, 16)
    shared["w_gate"] = _chunk_w(f(w_gate)[0], KC, FC)
    shared["w_up"] = _chunk_w(f(w_up)[0], KC, FC)
    shared["w_down"] = _chunk_w(f(w_down)[0], FC, 16)
    shared["sinkb"] = np.ascontiguousarray(np.broadcast_to(f(a_sink)[0][None, :], (128, 8)))
    shared["gainA"] = np.ascontiguousarray(np.broadcast_to(f(a_out_gain)[0][None, :], (128, 1024)))
    shared["gainB"] = np.ascontiguousarray(np.broadcast_to(f(diff_subln_gain)[0][None, :], (128, 128)))
    lq = np.stack([f(diff_lq1)[0], f(diff_lk1)[0], f(diff_lq2)[0], f(diff_lk2)[0]], axis=0)
    shared["lqk"] = np.ascontiguousarray(np.broadcast_to(lq[None], (128, 4, 64)))
    in_maps = []
    for core in range(8):
        b, half = core // 2, core % 2
        xb = x[b]
        if half == 1:
            xb = xb[::-1]
        xTb = np.ascontiguousarray(xb.T.reshape(KC, 128, SEQ).transpose(1, 0, 2))
        m = dict(shared)
        m["xT"] = xTb
        m["cT"] = _vec16(f(c)[b])
        in_maps.append(m)
    return in_maps


_NC_CACHE = {}


def kernel(**inputs):
    if "nc" not in _NC_CACHE:
        _NC_CACHE["nc"] = build_program()
    nc = _NC_CACHE["nc"]
    in_maps = make_in_maps(**inputs)
    res = run_bass_kernel_spmd(nc, in_maps, core_ids=list(range(8)))
    out = np.empty((4, SEQ, D), dtype=np.float32)
    for core in range(8):
        b, half = core // 2, core % 2
        oT = np.asarray(res.results[core]["outT"])
        o = oT.transpose(2, 1, 0).reshape(OWN, D)
        if half == 0:
            out[b, :OWN] = o
        else:
            out[b, OWN:] = o[::-1]
    return out
```

```python
import contextlib
import os
import numpy as np
import concourse.bass as bass
import concourse.mybir as mybir
from concourse.bass_utils import run_bass_kernel_spmd

F32 = mybir.dt.float32
BF16 = mybir.dt.bfloat16
AF = mybir.ActivationFunctionType
ALU = mybir.AluOpType
AX = mybir.AxisListType

D = 2048
SEQ = 8192
OWN = 4096
KC = 16
DFF = 5632
FC = 44
EPS = 1e-6
SCALE_A = 128.0 ** -0.5
SCALE_B = 64.0 ** -0.5
LAM_INIT = 0.2
SLOPES = [2.0 ** (-8.0 * h / 16.0) for h in range(1, 17)]
SLOPES_A = SLOPES[:8]
SLOPES_B = SLOPES[8:]
NEG = -30000.0


class Op:
    __slots__ = ("eng", "fn", "deps", "signal", "sigval", "is_dma", "slot")

    def __init__(self, eng, fn):
        self.eng = eng
        self.fn = fn
        self.deps = []
        self.signal = False
        self.sigval = None
        self.is_dma = False
        self.slot = None


class Sched:
    ENGS = ("pe", "act", "dve", "pool", "sp")

    def __init__(self, nc, stack, n_dma_slots=8):
        self.nc = nc
        self.n_slots = n_dma_slots
        self.sems = {}
        for e in self.ENGS:
            self.sems[e] = stack.enter_context(nc.semaphore("s_" + e))
        for e in ("sp", "pool"):
            for s in range(n_dma_slots):
                self.sems[(e, s)] = stack.enter_context(nc.semaphore("d_%s_%d" % (e, s)))
        self.cnt = {k: 0 for k in self.sems}
        self.slot_rr = {e: 0 for e in self.ENGS}
        self.slot_last = {}
        self.out_dma = []
        self._reset()

    def _reset(self):
        self.ops = {e: [] for e in self.ENGS}
        self.last_w = {}
        self.readers = {}

    def op(self, eng, fn, reads=(), writes=(), dma=False, is_output=False):
        o = Op(eng, fn)
        o.is_dma = dma
        deps = []
        for b in reads:
            w = self.last_w.get(b)
            if w is not None:
                deps.append(w)
        for b in writes:
            w = self.last_w.get(b)
            if w is not None:
                deps.append(w)
            deps.extend(self.readers.get(b, ()))
        if dma:
            s = self.slot_rr[eng]
            self.slot_rr[eng] = (s + 1) % self.n_slots
            o.slot = (eng, s)
            prev = self.slot_last.get(o.slot)
            if prev is not None:
                deps.append(prev)
            self.slot_last[o.slot] = o
            if is_output:
                self.out_dma.append(o)
        o.deps = deps
        self.ops[eng].append(o)
        for b in reads:
            lst = self.readers.setdefault(b, [])
            if not dma:
                for i, r in enumerate(lst):
                    if (not r.is_dma) and r.eng == eng:
                        del lst[i]
                        break
            lst.append(o)
        for b in writes:
            self.last_w[b] = o
            self.readers[b] = []
        return o

    def emit(self):
        nc = self.nc
        lasts = []
        for e in self.ENGS:
            comp = [o for o in self.ops[e] if not o.is_dma and o.fn is not None]
            if comp:
                lasts.append(comp[-1])
        lasts.extend([o for o in self.slot_last.values() if o is not None])
        for e in self.ENGS:
            fin = Op(e, None)
            fin.deps = [d for d in lasts]
            self.ops[e].append(fin)
        for e in self.ENGS:
            for o in self.ops[e]:
                for d in o.deps:
                    if d.sigval is not None:
                        continue
                    if d.is_dma:
                        d.signal = True
                    elif d.eng != o.eng or o.is_dma or o.fn is None:
                        d.signal = True
                    elif d.eng != "pe":
                        d.signal = True
        for e in self.ENGS:
            for o in self.ops[e]:
                if o.fn is None or o.sigval is not None:
                    continue
                if o.is_dma:
                    self.cnt[o.slot] += 16
                    o.sigval = self.cnt[o.slot]
                elif o.signal:
                    self.cnt[e] += 1
                    o.sigval = self.cnt[e]
        sems = self.sems

        def stream(e):
            ops = self.ops[e]

            def f(eng):
                seen = {}
                for o in ops:
                    need = {}
                    for d in o.deps:
                        if (not d.is_dma) and d.eng == e and (not o.is_dma) and o.fn is not None and e == "pe":
                            continue
                        key = d.slot if d.is_dma else d.eng
                        if need.get(key, 0) < d.sigval:
                            need[key] = d.sigval
                    for key, v in need.items():
                        if seen.get(key, 0) >= v:
                            continue
                        seen[key] = v
                        eng.wait_ge(sems[key], v)
                    if o.fn is None:
                        continue
                    ins = o.fn(eng)
                    if o.is_dma:
                        ins.then_inc(sems[o.slot], 16)
                    elif o.signal:
                        ins.then_inc(sems[e], 1)
            return f

        with nc.Block() as block:
            block.tensor(stream("pe"))
            block.scalar(stream("act"))
            block.vector(stream("dve"))
            block.gpsimd(stream("pool"))
            block.sync(stream("sp"))
        self._reset()


def build_program(stop_after=99, debug_outputs=False):
    nc = bass.Bass("TRN2", target_bir_lowering=False)

    def din(name, shape, dt=F32):
        return nc.dram_tensor(name, list(shape), dt, kind="ExternalInput").ap()

    def dscr(name, shape, dt=BF16):
        kind = "ExternalOutput" if debug_outputs else "Internal"
        return nc.dram_tensor(name, list(shape), dt, kind=kind).ap()

    xT = din("xT", [128, KC, SEQ])
    cT = din("cT", [128, KC])
    w_ada = din("w_ada", [96, 128, KC, 128])
    b_ada = din("b_ada", [128, 96])
    n1g = din("n1g", [128, KC])
    n2g = din("n2g", [128, KC])
    fgn = din("fgn", [128, KC])
    w_in = din("w_in", [36, 128, KC, 128])
    w_o = din("w_o", [16, 128, KC, 128])
    w_gate = din("w_gate", [FC, 128, KC, 128])
    w_up = din("w_up", [FC, 128, KC, 128])
    w_down = din("w_down", [16, 128, FC, 128])
    sinkb = din("sinkb", [128, 8])
    gainA = din("gainA", [128, 1024])
    gainB = din("gainB", [128, 128])
    lqk = din("lqk", [128, 4, 64])
    c_ident = din("c_ident", [128, 128])
    c_biasA = din("c_biasA", [128, 6, 512])
    c_tdist = din("c_tdist", [128, 896])
    c_biasB = din("c_biasB", [128, 8, 126])
    c_fbfa = din("c_fbfa", [128, 8, 8])
    outT = nc.dram_tensor("outT", [128, KC, OWN], F32, kind="ExternalOutput").ap()

    w_in_bf = dscr("w_in_bf", [36, 128, KC, 128])
    w_o_bf = dscr("w_o_bf", [16, 128, KC, 128])
    w_gate_bf = dscr("w_gate_bf", [FC, 128, KC, 128])
    w_up_bf = dscr("w_up_bf", [FC, 128, KC, 128])
    w_down_bf = dscr("w_down_bf", [16, 128, FC, 128])
    QaT = dscr("QaT", [8, 128, OWN])
    KaT = dscr("KaT", [2, 128, 5120])
    Va = dscr("Va", [128, 40, 2, 129])
    QbT = dscr("QbT", [8, 128, OWN])
    KbT = dscr("KbT", [8, 128, SEQ])
    Vb = dscr("Vb", [8, 128, 64, 129])
    mixT = dscr("mixT", [16, 128, OWN])

    top = contextlib.ExitStack()
    with top:
        S = Sched(nc, top)

        def sbt(st, name, shape, dt):
            return st.enter_context(nc.sbuf_tensor(name, list(shape), dt))

        pbS = top.enter_context(nc.psum_tensor("pbS", [128, 4, 512], F32))
        banks = [pbS[:, i, :] for i in range(4)]
        banks += [top.enter_context(nc.psum_tensor("pb%d" % i, [128, 512], F32))[:, :] for i in range(4, 7)]
        pbT = top.enter_context(nc.psum_tensor("pbT", [128, 1024], BF16))
        bank_rr = [0]

        def nextbank(nb=7, base=0):
            i = base + bank_rr[0] % nb
            bank_rr[0] += 1
            return banks[i], "pb%d" % i

        ident = sbt(top, "ident", [128, 128], BF16)
        ones = sbt(top, "ones", [128, 128], BF16)
        epsb = sbt(top, "epsb", [128, 1], F32)
        mod = sbt(top, "mod", [128, 96], F32)
        G1 = sbt(top, "G1", [128, KC], F32)
        G2 = sbt(top, "G2", [128, KC], F32)
        fg_sb = sbt(top, "fg_sb", [128, KC], F32)
        neglam = sbt(top, "neglam", [128, 1], F32)
        esink = sbt(top, "esink", [128, 8], F32)
        gainB8 = sbt(top, "gainB8", [128, 128], F32)
        SH1 = mod[:, 0:16]
        g1c = mod[:, 32:48]
        SH2 = mod[:, 48:64]
        g2c = mod[:, 80:96]

        def dma(eng, out, in_, reads=(), writes=(), is_output=False):
            return S.op(eng, lambda e, o=out, i=in_: e.dma_start(out=o, in_=i), reads=reads, writes=writes,
                        dma=True, is_output=is_output)

        def mm(out, lhsT, rhs, start, stop, reads, writes, skip=False):
            if skip:
                fn = lambda e, o=out, l=lhsT, r=rhs, a=start, b=stop: e.matmul(
                    o, lhsT=l, rhs=r, start=a, stop=b, skip_group_check=True)
            else:
                fn = lambda e, o=out, l=lhsT, r=rhs, a=start, b=stop: e.matmul(o, lhsT=l, rhs=r, start=a, stop=b)
            return S.op("pe", fn, reads=reads, writes=writes)

        def act(out, in_, func, reads, writes, bias=None, scale=1.0):
            if bias is None:
                fn = lambda e, o=out, i=in_, f=func, s=scale: e.activation(out=o, in_=i, func=f, scale=s)
            else:
                fn = lambda e, o=out, i=in_, f=func, s=scale, b=bias: e.activation(out=o, in_=i, func=f, bias=b, scale=s)
            return S.op("act", fn, reads=reads, writes=writes)

        def stt(eng, out, in0, scalar, in1, op0, op1, reads, writes):
            return S.op(eng, lambda e, o=out, a=in0, s=scalar, b=in1, p=op0, q=op1: e.scalar_tensor_tensor(
                out=o, in0=a, scalar=s, in1=b, op0=p, op1=q), reads=reads, writes=writes)

        def tt(eng, out, in0, in1, op, reads, writes):
            return S.op(eng, lambda e, o=out, a=in0, b=in1, p=op: e.tensor_tensor(out=o, in0=a, in1=b, op=p),
                        reads=reads, writes=writes)

        def tcopy(eng, out, in_, reads, writes):
            if eng == "act":
                return S.op("act", lambda e, o=out, i=in_: e.copy(out=o, in_=i), reads=reads, writes=writes)
            return S.op(eng, lambda e, o=out, i=in_: e.tensor_copy(out=o, in_=i), reads=reads, writes=writes)

        def rstd_ops(out, in_, scale, reads, writes):
            act(out, in_, AF.Ln, reads=list(reads) + ["epsb"], writes=writes, bias=epsb[:, 0:1], scale=scale)
            act(out, out, AF.Exp, reads=writes, writes=writes, scale=-0.5)

        st0 = contextlib.ExitStack()
        st = st0
        if True:
            c_f = sbt(st, "c_f", [128, KC], F32)
            c_bf = sbt(st, "c_bf", [128, KC], BF16)
            bada = sbt(st, "bada", [128, 96], F32)
            n1 = sbt(st, "n1", [128, KC], F32)
            n2 = sbt(st, "n2", [128, KC], F32)
            lq = sbt(st, "lq", [128, 4, 64], F32)
            lp = sbt(st, "lp", [128, 2, 64], F32)
            ls = sbt(st, "ls", [128, 2], F32)
            sk = sbt(st, "sk", [128, 8], F32)
            gb = sbt(st, "gb", [128, 128], F32)
            wa = [sbt(st, "wa%d" % i, [128, KC, 128], BF16) for i in range(3)]

            def conv(src, dst, n, step, key):
                for c0 in range(0, n, step):
                    c1 = min(n, c0 + step)
                    dma("pool", dst[c0:c1], src[c0:c1], writes=["%s%d" % (key, c0 // step)])

            SKIP = os.environ.get("KSKIP", "")
            if "A" not in SKIP:
                conv(w_in, w_in_bf, 36, 4, "cw_in")
            dma("pool", ident[:], c_ident, writes=["ident"])
            dma("sp", c_f[:], cT, writes=["c_f"])
            dma("sp", bada[:], b_ada, writes=["bada"])
            dma("sp", n1[:], n1g, writes=["n1"])
            dma("sp", n2[:], n2g, writes=["n2"])
            dma("sp", fg_sb[:], fgn, writes=["fg"])
            dma("sp", lq[:], lqk, writes=["lq"])
            dma("sp", sk[:], sinkb, writes=["sk"])
            dma("sp", gb[:], gainB, writes=["gb"])
            S.op("dve", lambda e: e.memset(ones[:], 1.0), writes=["ones"])
            S.op("dve", lambda e: e.memset(epsb[:], EPS), writes=["epsb"])
            act(c_bf[:], c_f[:], AF.Silu, reads=["c_f"], writes=["c_bf"])
            pm = banks[0]

            def ada_load(ec):
                dma("pool", wa[ec % 3][:], w_ada[ec], writes=["wa%d" % (ec % 3)])

            def ada_mm(ec):
                w = wa[ec % 3]
                wk = "wa%d" % (ec % 3)
                for kc in range(KC):
                    mm(pm[:, ec:ec + 1], w[:, kc, :], c_bf[:, kc:kc + 1], kc == 0, kc == KC - 1,
                       reads=[wk, "c_bf"], writes=["pb0"])

            def ada_chunk(ec):
                ada_load(ec)
                ada_mm(ec)

            for ec in range(32):
                ada_chunk(ec)
            tt("dve", mod[:, 0:32], pm[:, 0:32], bada[:, 0:32], ALU.add, reads=["pb0", "bada"], writes=["mod"])
            stt("dve", G1[:], mod[:, 16:32], 1.0, n1[:], ALU.add, ALU.mult, reads=["mod", "n1"], writes=["G1"])
            slots = [[] for _ in range(64 + 2)]
            for k_, ec in enumerate(range(32, 96)):
                slots[k_].append(lambda ec=ec: ada_load(ec))
                slots[k_ + 2].append(lambda ec=ec: ada_mm(ec))
            ada_q = [(lambda sl=sl: [f_() for f_ in sl]) for sl in slots]

            def ada_fin():
                tt("dve", mod[:, 32:96], pm[:, 32:96], bada[:, 32:96], ALU.add, reads=["pb0", "bada"], writes=["mod"])
                stt("dve", G2[:], mod[:, 64:80], 1.0, n2[:], ALU.add, ALU.mult, reads=["mod", "n2"], writes=["G2"])
            ada_q.append(ada_fin)
            tt("dve", lp[:, 0, :], lq[:, 0, :], lq[:, 1, :], ALU.mult, reads=["lq"], writes=["lp"])
            tt("dve", lp[:, 1, :], lq[:, 2, :], lq[:, 3, :], ALU.mult, reads=["lq"], writes=["lp"])
            S.op("dve", lambda e: e.reduce_sum(out=ls[:], in_=lp[:], axis=AX.X), reads=["lp"], writes=["ls"])
            act(ls[:], ls[:], AF.Exp, reads=["ls"], writes=["ls"])
            tt("dve", neglam[:], ls[:, 1:2], ls[:, 0:1], ALU.subtract, reads=["ls"], writes=["neglam"])
            S.op("dve", lambda e: e.tensor_scalar_add(out=neglam[:], in0=neglam[:], scalar1=-LAM_INIT),
                 reads=["neglam"], writes=["neglam"])
            act(esink[:], sk[:], AF.Exp, reads=["sk"], writes=["esink"])
            S.op("dve", lambda e: e.tensor_scalar_mul(out=gainB8[:], in0=gb[:], scalar1=1.0 - LAM_INIT),
                 reads=["gb"], writes=["gainB8"])
            if debug_outputs:
                dbg = nc.dram_tensor("dbg", [128, 256], F32, kind="ExternalOutput").ap()
                dma("sp", dbg[:, 0:96], mod[:], reads=["mod"])
                dma("sp", dbg[:, 96:112], G1[:], reads=["G1"])
                dma("sp", dbg[:, 112:128], G2[:], reads=["G2"])
                dma("sp", dbg[:, 129:137], esink[:], reads=["esink"])
                dma("sp", dbg[:, 137:139], ls[:], reads=["ls"])
            p1_on = stop_after >= 1 and "1" not in os.environ.get("KNOPH", "")
            if not p1_on:
                while ada_q:
                    ada_q.pop(0)()
                S.emit()
                st0.close()

        if stop_after >= 1 and "1" not in os.environ.get("KNOPH", ""):
            with contextlib.ExitStack() as st:
                hT = [sbt(st, "hT%d" % i, [128, KC, 1024], BF16) for i in range(2)]
                xs = [sbt(st, "xs%d" % i, [128, KC, 256], F32) for i in range(2)]
                sq = [sbt(st, "sq%d" % i, [128, KC, 256], BF16) for i in range(2)]
                rs = [sbt(st, "rs%d" % i, [128, 256], F32) for i in range(2)]
                htmp = [sbt(st, "htmp%d" % i, [128, 256], F32) for i in range(2)]
                wb = [sbt(st, "wb%d" % i, [128, KC, 128], BF16) for i in range(3)]
                wv = [sbt(st, "wv%d" % i, [128, 4, KC, 128], BF16) for i in range(2)]
                stg = [sbt(st, "stg%d" % i, [128, 512], BF16) for i in range(4)]
                vst = [sbt(st, "vst%d" % i, [128, 4, 129], BF16) for i in range(4)]
                for i in range(4):
                    S.op("dve", lambda e, t=vst[i]: e.memset(t[:], 1.0), writes=["vst%d" % i])
                cnt = {"sg": 0, "w": 0, "wv": 0, "stg": 0, "vst": 0, "ev": 0}

                sg_pending = {}

                def subgroup_a(e8, sgi):
                    t0 = e8 * 1024 + sgi * 256
                    i = cnt["sg"] % 2
                    cnt["sg"] += 1
                    x_, q_ = xs[i], sq[i]
                    xk, qk = "xs%d" % i, "sq%d" % i
                    dma("sp", x_[:], xT[:, :, t0:t0 + 256], writes=[xk])
                    act(q_[:], x_[:], AF.Square, reads=[xk], writes=[qk])
                    sg_pending[(e8, sgi)] = i

                def subgroup(e8, sgi):
                    if (e8, sgi) not in sg_pending:
                        subgroup_a(e8, sgi)
                    hb = e8 % 2
                    i = sg_pending.pop((e8, sgi))
                    x_, q_, r_ = xs[i], sq[i], rs[i]
                    xk, qk, rk = "xs%d" % i, "sq%d" % i, "rs%d" % i
                    bk, bkey = nextbank(6, 1)
                    for kc in range(KC):
                        mm(bk[:, 0:256], ones[:], q_[:, kc, :], kc == 0, kc == KC - 1, reads=[qk, "ones"], writes=[bkey])
                    rstd_ops(r_[:], bk[:, 0:256], 1.0 / D, reads=[bkey], writes=[rk])
                    hk = "hT%d_%d" % (hb, sgi // 2)
                    for kc in range(KC):
                        tm = htmp[kc % 2]
                        tk = "htmp%d" % (kc % 2)
                        stt("dve", tm[:], x_[:, kc, :], G1[:, kc:kc + 1], r_[:], ALU.mult, ALU.mult,
                            reads=[xk, rk, "G1"], writes=[tk])
                        act(hT[hb][:, kc, sgi * 256:(sgi + 1) * 256], tm[:], AF.Identity, reads=[tk, "mod"],
                            writes=[hk], bias=SH1[:, kc:kc + 1])

                def evac(out, in_, reads, writes):
                    eng = "act" if cnt["ev"] % 2 == 0 else "dve"
                    cnt["ev"] += 1
                    tcopy(eng, out, in_, reads, writes)

                def fm_chunk(e8, cc, dst, tgs):
                    hb = e8 % 2
                    i = cnt["w"] % 3
                    cnt["w"] += 1
                    w, wk = wb[i], "wb%d" % i
                    dma("sp", w[:], w_in_bf[cc], reads=["cw_in%d" % (cc // 4)], writes=[wk])
                    bl = [nextbank(6, 1) for _ in tgs]
                    for kc in range(KC):
                        for j, tg in enumerate(tgs):
                            mm(bl[j][0][:, :], w[:, kc, :], hT[hb][:, kc, tg * 512:(tg + 1) * 512], kc == 0, kc == KC - 1,
                               reads=[wk, "hT%d_%d" % (hb, tg)], writes=[bl[j][1]])
                    for j, tg in enumerate(tgs):
                        si = cnt["stg"] % 4
                        cnt["stg"] += 1
                        evac(stg[si][:], bl[j][0][:, :], reads=[bl[j][1]], writes=["stg%d" % si])
                        t0 = e8 * 1024 + tg * 512
                        dma("pool", dst[:, t0:t0 + 512], stg[si][:], reads=["stg%d" % si])

                def tm_chunks(e8, c0, ncol, dstfn, tbs):
                    hb = e8 % 2
                    i = cnt["wv"] % 2
                    cnt["wv"] += 1
                    w, wk = wv[i], "wv%d" % i
                    dma("sp", w[:, 0:ncol, :, :], w_in_bf[c0:c0 + ncol].rearrange("c p k j -> p c k j"),
                        reads=["cw_in%d" % (c0 // 4)], writes=[wk])
                    for tb in tbs:
                        bk, bkey = nextbank(6, 1)
                        o = bk[:, 0:ncol * 128].rearrange("p (c j) -> p c j", j=128)
                        for kc in range(KC):
                            mm(o, hT[hb][:, kc, tb * 128:(tb + 1) * 128], w[:, 0:ncol, kc, :], kc == 0, kc == KC - 1,
                               reads=[wk, "hT%d_%d" % (hb, tb // 4)], writes=[bkey])
                        vi = cnt["vst"] % 4
                        cnt["vst"] += 1
                        evac(vst[vi][:, 0:ncol, 0:128], o, reads=[bkey], writes=["vst%d" % vi])
                        kb = e8 * 8 + tb
                        dma("pool", dstfn(kb), vst[vi][:, 0:ncol, :], reads=["vst%d" % vi])

                for sgi in range(4):
                    subgroup(0, sgi)
                for e8 in range(8):
                    own = e8 < 4
                    work = []
                    if own:
                        for cc in range(8):
                            work.append(("fm", cc, QaT[cc], (0, 1)))
                        for cc in (8, 9):
                            work.append(("fm", cc, KaT[cc - 8], (0, 1)))
                        work.append(("tm", 10, 2, (lambda kb: Va[:, kb, :, :]), tuple(range(8))))
                        for cc in range(12, 20):
                            work.append(("fm", cc, QbT[cc - 12], (0, 1)))
                    elif e8 == 4:
                        for cc in (8, 9):
                            work.append(("fm", cc, KaT[cc - 8], (0,)))
                        work.append(("tm", 10, 2, (lambda kb: Va[:, kb, :, :]), (0,)))
                    for cc in range(20, 28):
                        work.append(("fm", cc, KbT[cc - 20], (0, 1)))
                    for hf in range(2):
                        work.append(("tm", 28 + 4 * hf, 4,
                                     (lambda kb, hf=hf: Vb[4 * hf:4 * hf + 4, :, kb, :].rearrange("h p e -> p h e")),
                                     tuple(range(8))))
                    nxt = 0
                    for wi, item in enumerate(work):
                        if item[0] == "fm":
                            fm_chunk(e8, item[1], item[2], item[3])
                        else:
                            tm_chunks(e8, item[1], item[2], item[3], item[4])
                        if e8 + 1 < 8 and nxt < 4 and wi % 2 == 0 and wi >= 2:
                            subgroup_a(e8 + 1, nxt)
                        if e8 + 1 < 8 and nxt < 4 and wi % 2 == 1 and wi >= 3:
                            subgroup(e8 + 1, nxt)
                            nxt += 1
                        if ada_q:
                            ada_q.pop(0)()
                    while e8 + 1 < 8 and nxt < 4:
                        subgroup(e8 + 1, nxt)
                        nxt += 1
                while ada_q:
                    ada_q.pop(0)()
                S.emit()
            st0.close()

        if stop_after >= 2:
            with contextlib.ExitStack() as st:
                qa = sbt(st, "qa", [128, 8, OWN], BF16)
                ka = sbt(st, "ka", [128, 2, 4224], BF16)
                va = sbt(st, "va", [128, 33, 2, 129], BF16)
                bA = sbt(st, "bA", [128, 6, 512], F32)
                gA = sbt(st, "gA", [128, 1024], F32)
                tmpA = [sbt(st, "tmpA%d" % i, [128, 512], F32) for i in range(2)]
                ptA = [sbt(st, "ptA%d" % i, [128, 512], BF16) for i in range(2)]
                oa = sbt(st, "oa", [128, 1024], F32)
                oq = sbt(st, "oq", [128, 1024], F32)
                oab = sbt(st, "oab", [128, 1024], BF16)
                zz = sbt(st, "zz", [128, 8], F32)
                ssA = sbt(st, "ssA", [128, 1], F32)
                mst = [sbt(st, "mst%d" % i, [128, 8, 512], BF16) for i in range(2)]
                for h_ in range(8):
                    dma("sp", qa[:, h_, :], QaT[h_], writes=["qa"])
                dma("sp", ka[:], KaT[:, :, 0:4224].rearrange("g p t -> p g t"), writes=["ka"])
                dma("sp", va[:], Va[:, 0:33, :, :], writes=["va"])
                dma("sp", bA[:], c_biasA, writes=["bA"])
                dma("sp", gA[:], gainA, writes=["gA"])
                def accA(h):
                    return banks[4 + h // 3][:, (h % 3) * 129:(h % 3) * 129 + 129], "pb%d" % (4 + h // 3)
                accsb = [sbt(st, "accsb%d" % i, [128, 8, 129], F32) for i in range(2)]
                NB2 = int(os.environ.get("KN2", "32"))
                steps = []
                for n in range(NB2):
                    for g in range(2):
                        ts = [t for t in range(3) if 0 <= n - 1 + t <= 32]
                        for ti, t in enumerate(ts):
                            steps.append((n, g, t, ti == 0, ti == len(ts) - 1, g == 1 and ti == len(ts) - 1))
                import collections
                deferA = collections.deque()

                def emit_sA(step):
                    n, g, t, first, last, blk_end = step
                    kb = n - 1 + t
                    bk, bkey = nextbank(4)
                    o = bk[:, :].rearrange("p (h q) -> p h q", q=128)
                    mm(o, ka[:, g, kb * 128:(kb + 1) * 128], qa[:, 4 * g:4 * g + 4, n * 128:(n + 1) * 128], True, True,
                       reads=["ka", "qa"], writes=[bkey])
                    return bk, bkey

                def chain(n, asb, ask):
                    ops = []
                    ops.append(lambda: tt("dve", zz[:], asb[:, :, 128], esink[:], ALU.add, reads=[ask, "esink"], writes=["zz"]))
                    ops.append(lambda: S.op("dve", lambda e: e.reciprocal(out=zz[:], in_=zz[:]), reads=["zz"], writes=["zz"]))
                    for h in range(8):
                        ops.append(lambda h=h: S.op(
                            "dve", lambda e, o=oa[:, h * 128:(h + 1) * 128], a=asb[:, h, 0:128], s_=zz[:, h:h + 1]:
                            e.tensor_scalar_mul(out=o, in0=a, scalar1=s_), reads=[ask, "zz"], writes=["oa"]))
                    ops.append(lambda: tt("dve", oq[:], oa[:], oa[:], ALU.mult, reads=["oa"], writes=["oq"]))
                    ops.append(lambda: S.op("dve", lambda e: e.reduce_sum(out=ssA[:], in_=oq[:], axis=AX.X),
                                            reads=["oq"], writes=["ssA"]))
                    ops.append(lambda: act(ssA[:], ssA[:], AF.Ln, reads=["ssA", "epsb"], writes=["ssA"],
                                           bias=epsb[:, 0:1], scale=1.0 / 1024.0))
                    ops.append(lambda: act(ssA[:], ssA[:], AF.Exp, reads=["ssA"], writes=["ssA"], scale=-0.5))
                    ops.append(lambda: stt("dve", oab[:], oa[:], ssA[:, 0:1], gA[:], ALU.mult, ALU.mult,
                                           reads=["oa", "ssA", "gA"], writes=["oab"]))
                    ms = mst[(n // 4) % 2]
                    mk = "mst%d" % ((n // 4) % 2)

                    def tr_all():
                        for h in range(8):
                            pslot = pbT[:, h * 128:h * 128 + 128]
                            S.op("pe", lambda e, o=pslot, i_=oab[:, h * 128:(h + 1) * 128]: e.transpose(o, i_, ident[:]),
                                 reads=["oab", "ident"], writes=["pbT"])
                    ops.append(tr_all)
                    ops.append(lambda: tcopy("dve", ms[:, :, (n % 4) * 128:(n % 4) * 128 + 128],
                                             pbT[:, :].rearrange("p (h q) -> p h q", q=128), reads=["pbT"], writes=[mk]))
                    if n % 4 == 3:
                        t0 = (n // 4) * 512
                        ops.append(lambda: dma("pool", mixT[0:8, :, t0:t0 + 512].rearrange("c p t -> p c t"), ms[:], reads=[mk]))
                    return ops

                sq_ = [emit_sA(st_) for st_ in steps[:2]]
                for idx, step in enumerate(steps):
                    n, g, t, first, last, blk_end = step
                    kb = n - 1 + t
                    bk, bkey = sq_.pop(0)
                    i = idx % 2
                    stt("dve", tmpA[i][:], bk[:, :], SCALE_A, bA[:, g * 3 + t, :], ALU.mult, ALU.add,
                        reads=[bkey, "bA"], writes=["tmpA%d" % i])
                    act(ptA[i][:], tmpA[i][:], AF.Exp, reads=["tmpA%d" % i], writes=["ptA%d" % i])
                    if idx + 2 < len(steps):
                        sq_.append(emit_sA(steps[idx + 2]))
                    for hh in range(4):
                        h = 4 * g + hh
                        ao, ak = accA(h)
                        mm(ao, ptA[i][:, hh * 128:(hh + 1) * 128], va[:, kb, g, :], first and h in (0, 3, 6),
                           last, reads=["ptA%d" % i, "va"], writes=[ak], skip=True)
                    if blk_end:
                        asb, ask = accsb[n % 2], "accsb%d" % (n % 2)
                        for b_ in range(3):
                            nh = 3 if b_ < 2 else 2
                            tcopy("dve", asb[:, 3 * b_:3 * b_ + nh, :],
                                  banks[4 + b_][:, 0:nh * 129].rearrange("p (h e) -> p h e", e=129),
                                  reads=["pb%d" % (4 + b_)], writes=[ask])
                        deferA.extend(chain(n, asb, ask))
                    for _ in range(4):
                        if deferA:
                            deferA.popleft()()
                while deferA:
                    deferA.popleft()()
                S.emit()

        if stop_after >= 3:
            with contextlib.ExitStack() as st:
                kbt = [sbt(st, "kbt%d" % i, [128, SEQ], BF16) for i in range(2)]
                vbt = [sbt(st, "vbt%d" % i, [128, 64, 129], BF16) for i in range(2)]
                qbt = [sbt(st, "qbt%d" % i, [128, OWN], BF16) for i in range(2)]
                td = sbt(st, "td", [128, 896], F32)
                bB = sbt(st, "bB", [128, 8, 126], F32)
                ff = sbt(st, "ff", [128, 8, 8], F32)
                tmpD = [sbt(st, "tmpD%d" % i, [128, 2, 512], F32) for i in range(3)]
                dcnt = [0]
                pt = [sbt(st, "pt%d" % i, [128, 2, 512], BF16) for i in range(4)]
                accS_sets = [[sbt(st, "accS%d_%d" % (k_, i), [128, 129], F32) for i in range(8)] for k_ in range(2)]
                accS = accS_sets[0]
                cset = [0]
                import collections
                defer = collections.deque()
                rr = sbt(st, "rr", [128, 2], F32)
                uu = sbt(st, "uu", [128, 1], F32)
                od = sbt(st, "od", [128, 128], F32)
                oq2 = sbt(st, "oq2", [128, 128], F32)
                ssB = sbt(st, "ssB", [128, 1], F32)
                ob = sbt(st, "ob", [128, 128], BF16)
                obT = [sbt(st, "obT%d" % i, [128, 512], BF16) for i in range(2)]
                dma("sp", td[:], c_tdist, writes=["td"])
                dma("sp", bB[:], c_biasB, writes=["bB"])
                dma("sp", ff[:], c_fbfa, writes=["ff"])

                def accB(j):
                    return banks[4 + j // 3][:, (j % 3) * 129:(j % 3) * 129 + 129], "pb%d" % (4 + j // 3)

                later_conv = []
                for src_, dst_, n_, step_ in ((w_o, w_o_bf, 16, 4), (w_gate, w_gate_bf, FC, 4), (w_up, w_up_bf, FC, 4),
                                              (w_down, w_down_bf, 16, 1)):
                    for c0 in range(0, n_, step_):
                        c1 = min(n_, c0 + step_)
                        later_conv.append((src_[c0:c1], dst_[c0:c1]))
                junk = pbT.bitcast(F32)[:, 256:512]
                NJUNK = int(os.environ.get("KJUNK", "0"))
                sbank = [0]

                def next_s():
                    i = sbank[0] % 4
                    sbank[0] += 1
                    return banks[i], "pb%d" % i

                pti = [0]
                fin = [0]
                for h in range(int(os.environ.get("KH3", "8"))):
                    hb = h % 2
                    kk, vk, qk = "kbt%d" % hb, "vbt%d" % hb, "qbt%d" % hb
                    dma("sp", kbt[hb][:], KbT[h], writes=[kk])
                    dma("sp", vbt[hb][:], Vb[h], writes=[vk])
                    dma("sp", qbt[hb][:], QbT[h], writes=[qk])
                    K_, V_, Q_ = kbt[hb], vbt[hb], qbt[hb]
                    slope = SLOPES_B[h]
                    items = []
                    for qg in range(int(os.environ.get("KQG3", "8"))):
                        diag = list(range(4 * qg, 4 * qg + 4))
                        before = list(range(0, 4 * qg))
                        after = list(range(4 * qg + 4, 64))
                        for ph, lst in (("d", diag), ("b", before), ("a", after)):
                            for ii, kb in enumerate(lst):
                                items.append((qg, ph, kb, ii == 0, ii == len(lst) - 1))

                    def emit_scores(item):
                        qg, ph, kb, first, last = item
                        sp_ = sbank[0] % 2
                        sbank[0] += 1
                        for c in range(2):
                            mm(pbS[:, 2 * sp_ + c, :], K_[64 * c:64 * c + 64, kb * 128:(kb + 1) * 128],
                               Q_[64 * c:64 * c + 64, qg * 512:(qg + 1) * 512], True, True,
                               reads=[kk, qk], writes=["pb%d" % (2 * sp_ + c)])
                        di = None
                        if ph == "d":
                            di = dcnt[0] % 3
                            dcnt[0] += 1
                            o_ = kb - 4 * qg
                            for c in range(2):
                                stt("dve", tmpD[di][:, c, :], td[:, 384 - 128 * o_:896 - 128 * o_], -slope / SCALE_B,
                                    pbS[:, 2 * sp_ + c, :], ALU.mult, ALU.add, reads=["td", "pb%d" % (2 * sp_ + c)],
                                    writes=["tmpD%d" % di])
                        return sp_, di

                    sq_ = [emit_scores(items[0]), emit_scores(items[1])]
                    for idx, item in enumerate(items):
                        qg, ph, kb, first, last = item
                        cur, di = sq_.pop(0)
                        pi = pti[0] % 4
                        pti[0] += 1
                        P_, pk = pt[pi], "pt%d" % pi
                        skeys = ["pb%d" % (2 * cur), "pb%d" % (2 * cur + 1)]
                        spair = pbS[:, 2 * cur:2 * cur + 2, :]
                        if ph == "d":
                            act(P_[:], tmpD[di][:], AF.Exp, reads=["tmpD%d" % di] + skeys, writes=[pk], scale=SCALE_B)
                        elif ph == "b":
                            d_ = 4 * qg - kb
                            act(P_[:], spair, AF.Exp, reads=skeys + ["bB"], writes=[pk],
                                bias=bB[:, h, d_ - 1:d_], scale=SCALE_B)
                        else:
                            d_ = kb - (4 * qg + 4)
                            act(P_[:], spair, AF.Exp, reads=skeys + ["bB"], writes=[pk],
                                bias=bB[:, h, 63 + d_:64 + d_], scale=SCALE_B)
                        if idx + 2 < len(items):
                            sq_.append(emit_scores(items[idx + 2]))
                        for c in range(2):
                            for a in range(4):
                                ao, ak = accB(2 * a + c)
                                mm(ao, P_[:, c, a * 128:(a + 1) * 128], V_[:, kb, :], first and (2 * a + c) in (0, 4, 6), last,
                                   reads=[pk, vk], writes=[ak], skip=True)
                        for _ in range(NJUNK):
                            mm(junk, ones[:], K_[:, kb * 128:kb * 128 + 256] if kb < 63 else K_[:, 0:256], True, True,
                               reads=["ones", kk], writes=["pbT"], skip=True)
                        if last:
                            for a in range(4):
                                for c in range(2):
                                    j = 2 * a + c
                                    ao, ak = accB(j)
                                    sk2 = "accSs%d_%d" % (cset[0], j)
                                    if ph == "d":
                                        tcopy("dve", accS[j][:], ao, reads=[ak], writes=[sk2])
                                    else:
                                        col = 2 * a + (0 if ph == "b" else 1)
                                        stt("dve", accS[j][:], ao, ff[:, h, col:col + 1], accS[j][:], ALU.mult, ALU.add,
                                            reads=[ak, "ff", sk2], writes=[sk2])
                            lastphase = (ph == "a")
                            if lastphase:
                                oi = fin[0] % 2
                                fin[0] += 1
                                OT, otk = obT[oi], "obT%d" % oi
                                aset = accS
                                sk_ = "s%d_" % cset[0]
                                for a in range(4):
                                    a0, a1 = aset[2 * a], aset[2 * a + 1]
                                    k0, k1 = "accS%s%d" % (sk_, 2 * a), "accS%s%d" % (sk_, 2 * a + 1)
                                    defer.append(lambda x=a0, k0=k0: S.op(
                                        "dve", lambda e, x=x: e.reciprocal(out=rr[:, 0:1], in_=x[:, 128:129]),
                                        reads=[k0], writes=["rr"]))
                                    defer.append(lambda x=a1, k1=k1: S.op(
                                        "dve", lambda e, x=x: e.reciprocal(out=rr[:, 1:2], in_=x[:, 128:129]),
                                        reads=[k1], writes=["rr"]))
                                    defer.append(lambda: tt("dve", uu[:], rr[:, 1:2], neglam[:], ALU.mult,
                                                            reads=["rr", "neglam"], writes=["uu"]))
                                    defer.append(lambda x=a0, k0=k0: S.op(
                                        "dve", lambda e, x=x: e.tensor_scalar_mul(out=od[:], in0=x[:, 0:128], scalar1=rr[:, 0:1]),
                                        reads=[k0, "rr"], writes=["od"]))
                                    defer.append(lambda x=a1, k1=k1: stt("dve", od[:], x[:, 0:128], uu[:, 0:1], od[:], ALU.mult, ALU.add,
                                                                         reads=[k1, "uu", "od"], writes=["od"]))
                                    defer.append(lambda: tt("dve", oq2[:], od[:], od[:], ALU.mult, reads=["od"], writes=["oq2"]))
                                    defer.append(lambda: S.op("dve", lambda e: e.reduce_sum(out=ssB[:], in_=oq2[:], axis=AX.X),
                                                              reads=["oq2"], writes=["ssB"]))
                                    defer.append(lambda: act(ssB[:], ssB[:], AF.Ln, reads=["ssB", "epsb"], writes=["ssB"],
                                                             bias=epsb[:, 0:1], scale=1.0 / 128.0))
                                    defer.append(lambda: act(ssB[:], ssB[:], AF.Exp, reads=["ssB"], writes=["ssB"], scale=-0.5))
                                    defer.append(lambda: stt("dve", ob[:], od[:], ssB[:, 0:1], gainB8[:], ALU.mult, ALU.mult,
                                                             reads=["od", "ssB", "gainB8"], writes=["ob"]))
                                    pslot = pbT[:, a * 128:(a + 1) * 128]
                                    defer.append(lambda o=pslot: S.op("pe", lambda e, o=o: e.transpose(o, ob[:], ident[:]),
                                                                      reads=["ob", "ident"], writes=["pbT"]))
                                    defer.append(lambda o=pslot, OT=OT, otk=otk, a=a: tcopy(
                                        "dve", OT[:, a * 128:(a + 1) * 128], o, reads=["pbT"], writes=[otk]))

                                def _store(OT=OT, otk=otk, h=h, qg=qg):
                                    dma("pool", mixT[8 + h][:, qg * 512:(qg + 1) * 512], OT[:], reads=[otk])
                                    if later_conv:
                                        src_, dst_ = later_conv.pop(0)
                                        dma("pool", dst_, src_)
                                defer.append(_store)
                                cset[0] ^= 1
                                accS = accS_sets[cset[0]]
                        if defer:
                            defer.popleft()()
                while defer:
                    defer.popleft()()
                while later_conv:
                    src_, dst_ = later_conv.pop(0)
                    dma("pool", dst_, src_)
                S.emit()

        if stop_after >= 4:
            with contextlib.ExitStack() as st:
                xts = [sbt(st, "xt%d" % i, [128, KC, 512], F32) for i in range(2)]
                mx = sbt(st, "mx", [128, KC, 512], BF16)
                h2 = sbt(st, "h2", [128, KC, 512], BF16)
                ffT = sbt(st, "ffT", [128, FC, 512], BF16)
                w16 = [sbt(st, "w16_%d" % i, [128, KC, 128], BF16) for i in range(4)]
                wd = [sbt(st, "wd%d" % i, [128, FC, 128], BF16) for i in range(2)]
                sg = [sbt(st, "sg%d" % i, [128, 512], F32) for i in range(2)]
                rsA = sbt(st, "rsA", [128, 512], F32)
                rsB = sbt(st, "rsB", [128, 512], F32)
                ht4 = [sbt(st, "ht4_%d" % i, [128, 512], F32) for i in range(2)]
                sqs = [sbt(st, "sqs%d" % i, [128, 512], BF16) for i in range(3)]
                wcnt = [0]
                sqc = [0]
                NTG = int(os.environ.get("KTG4", "8"))
                SSA, SSB = banks[6], banks[5]

                def loadw(src, key):
                    i = wcnt[0] % 4
                    wcnt[0] += 1
                    dma("sp", w16[i][:], src, reads=[key], writes=["w16_%d" % i])
                    return w16[i], "w16_%d" % i

                def sq_and_ss(xt_, xk, dc, ssb, sskey, pend):
                    i = sqc[0] % 3
                    sqc[0] += 1
                    act(sqs[i][:], xt_[:, dc, :], AF.Square, reads=[xk], writes=["sqs%d" % i])
                    pend.append(lambda i=i, dc=dc: mm(ssb[:, :], ones[:], sqs[i][:], dc == 0, dc == KC - 1,
                                                      reads=["sqs%d" % i, "ones"], writes=[sskey]))

                def WO(tg):
                    t0 = tg * 512
                    xt_, xk = xts[tg % 2], "xt%d" % (tg % 2)
                    dma("sp", xt_[:], xT[:, :, t0:t0 + 512], writes=[xk])
                    dma("sp", mx[:], mixT[:, :, t0:t0 + 512].rearrange("c p t -> p c t"), writes=["mx"])
                    pend = []
                    for dc in range(16):
                        w, wk = loadw(w_o_bf[dc], "cw_o%d" % (dc // 4))
                        bk, bkey = nextbank(5)
                        for kc in range(KC):
                            mm(bk[:, :], w[:, kc, :], mx[:, kc, :], kc == 0, kc == KC - 1, reads=[wk, "mx"], writes=[bkey])
                        stt("dve", xt_[:, dc, :], bk[:, :], g1c[:, dc:dc + 1], xt_[:, dc, :], ALU.mult, ALU.add,
                            reads=[bkey, "mod", xk], writes=[xk])
                        sq_and_ss(xt_, xk, dc, SSA, "pb6", pend)
                        if len(pend) > 2:
                            pend.pop(0)()
                    return pend

                def NORM2_ops(tg, pend):
                    xt_, xk = xts[tg % 2], "xt%d" % (tg % 2)
                    ops = []
                    for p_ in pend:
                        ops.append(p_)
                    ops.append(lambda: rstd_ops(rsA[:], SSA[:, :], 1.0 / D, reads=["pb6"], writes=["rsA"]))
                    for kc in range(KC):
                        def f(kc=kc):
                            tm, tk = ht4[kc % 2], "ht4_%d" % (kc % 2)
                            stt("dve", tm[:], xt_[:, kc, :], G2[:, kc:kc + 1], rsA[:], ALU.mult, ALU.mult,
                                reads=[xk, "rsA", "G2"], writes=[tk])
                            act(h2[:, kc, :], tm[:], AF.Identity, reads=[tk, "mod"], writes=["h2"], bias=SH2[:, kc:kc + 1])
                        ops.append(f)
                    return ops

                def FINAL_ops(tg, pend):
                    t0 = tg * 512
                    xt_, xk = xts[tg % 2], "xt%d" % (tg % 2)
                    ops = list(pend)
                    ops.append(lambda: rstd_ops(rsB[:], SSB[:, :], 1.0 / D, reads=["pb5"], writes=["rsB"]))
                    for kc in range(KC):
                        ops.append(lambda kc=kc: stt("dve", xt_[:, kc, :], xt_[:, kc, :], fg_sb[:, kc:kc + 1], rsB[:], ALU.mult, ALU.mult,
                                                     reads=[xk, "rsB", "fg"], writes=[xk]))
                    ops.append(lambda: dma("pool", outT[:, :, t0:t0 + 512], xt_[:], reads=[xk], is_output=True))
                    return ops

                pend = WO(0)
                for f_ in NORM2_ops(0, pend):
                    f_()
                fin_ops = []
                for tg in range(NTG):
                    xt_, xk = xts[tg % 2], "xt%d" % (tg % 2)
                    for fc in range(FC):
                        wg, wgk = loadw(w_gate_bf[fc], "cw_gate%d" % (fc // 4))
                        wu, wuk = loadw(w_up_bf[fc], "cw_up%d" % (fc // 4))
                        bg, bgk = nextbank(5)
                        bu, buk = nextbank(5)
                        for kc in range(KC):
                            mm(bg[:, :], wg[:, kc, :], h2[:, kc, :], kc == 0, kc == KC - 1, reads=[wgk, "h2"], writes=[bgk])
                        for kc in range(KC):
                            mm(bu[:, :], wu[:, kc, :], h2[:, kc, :], kc == 0, kc == KC - 1, reads=[wuk, "h2"], writes=[buk])
                        s_, sk_ = sg[fc % 2], "sg%d" % (fc % 2)
                        act(s_[:], bg[:, :], AF.Silu, reads=[bgk], writes=[sk_])
                        tt("dve", ffT[:, fc, :], s_[:], bu[:, :], ALU.mult, reads=[sk_, buk], writes=["ffT"])
                        if fin_ops:
                            fin_ops.pop(0)()
                    while fin_ops:
                        fin_ops.pop(0)()
                    n2_ops = []
                    if tg + 1 < NTG:
                        n2_ops = NORM2_ops(tg + 1, WO(tg + 1))
                    pend = []
                    for dc in range(16):
                        i = dc % 2
                        dma("sp", wd[i][:], w_down_bf[dc], reads=["cw_down%d" % dc], writes=["wd%d" % i])
                        bk, bkey = nextbank(5)
                        for fc in range(FC):
                            mm(bk[:, :], wd[i][:, fc, :], ffT[:, fc, :], fc == 0, fc == FC - 1,
                               reads=["wd%d" % i, "ffT"], writes=[bkey])
                        stt("dve", xt_[:, dc, :], bk[:, :], g2c[:, dc:dc + 1], xt_[:, dc, :], ALU.mult, ALU.add,
                            reads=[bkey, "mod", xk], writes=[xk])
                        sq_and_ss(xt_, xk, dc, SSB, "pb5", pend)
                        if len(pend) > 2:
                            pend.pop(0)()
                        for _ in range(2):
                            if n2_ops:
                                n2_ops.pop(0)()
                    while n2_ops:
                        n2_ops.pop(0)()
                    fin_ops = FINAL_ops(tg, pend)
                while fin_ops:
                    fin_ops.pop(0)()
                S.emit()
    return nc


def _consts():
    p = np.arange(128, dtype=np.float64)[:, None]
    f = np.arange(128, dtype=np.float64)[None, :]
    biasA = np.zeros((128, 6, 512), dtype=np.float32)
    for g in range(2):
        for t in range(3):
            for hh in range(4):
                sl = SLOPES_A[4 * g + hh]
                if t == 0:
                    dist = 128 + f - p
                    valid = f <= p
                elif t == 1:
                    dist = np.abs(f - p)
                    valid = np.ones_like(dist, dtype=bool)
                else:
                    dist = 128 + p - f
                    valid = p <= f
                b = np.where(valid, -sl * dist, NEG)
                biasA[:, g * 3 + t, hh * 128:(hh + 1) * 128] = b.astype(np.float32)
    gg = np.arange(896, dtype=np.float64)[None, :]
    tdist = np.abs(gg - 384 - p).astype(np.float32)
    biasB = np.zeros((128, 8, 126), dtype=np.float32)
    fbfa = np.zeros((128, 8, 8), dtype=np.float32)
    pp = np.arange(128, dtype=np.float64)
    for h in range(8):
        sl = SLOPES_B[h]
        for d in range(1, 64):
            biasB[:, h, d - 1] = -sl * (128 * d - pp)
        for d in range(0, 63):
            biasB[:, h, 63 + d] = -sl * (128 * d + pp + 1)
        for a in range(4):
            fbfa[:, h, 2 * a] = np.exp(-sl * (128 * a + pp))
            fbfa[:, h, 2 * a + 1] = np.exp(-sl * (511 - 128 * a - pp))
    return {
        "c_ident": np.eye(128, dtype=np.float32),
        "c_biasA": biasA,
        "c_tdist": tdist,
        "c_biasB": biasB,
        "c_fbfa": fbfa,
    }


def _chunk_w(w, kc, nc_):
    return np.ascontiguousarray(w.reshape(kc, 128, nc_, 128).transpose(2, 1, 0, 3))


def _vec16(v):
    return np.ascontiguousarray(v.reshape(KC, 128).T)


def make_in_maps(x, c, w_ada, b_ada, norm1_gain, w_in, a_sink, a_out_gain, diff_lq1, diff_lk1, diff_lq2, diff_lk2,
                 diff_subln_gain, w_o, norm2_gain, w_gate, w_up, w_down, final_gain):
    f = lambda a: np.asarray(a, dtype=np.float32)
    x = f(x)
    shared = dict(_consts())
    shared["w_ada"] = _chunk_w(f(w_ada)[0], KC, 96)
    shared["b_ada"] = np.ascontiguousarray(f(b_ada)[0].reshape(96, 128).T)
    shared["n1g"] = _vec16(f(norm1_gain)[0])
    shared["n2g"] = _vec16(f(norm2_gain)[0])
    shared["fgn"] = _vec16(f(final_gain))
    shared["w_in"] = _chunk_w(f(w_in)[0], KC, 36)
    shared["w_o"] = _chunk_w(f(w_o)[0], KC, 16)
    shared["w_gate"] = _chunk_w(f(w_gate)[0], KC, FC)
    shared["w_up"] = _chunk_w(f(w_up)[0], KC, FC)
    shared["w_down"] = _chunk_w(f(w_down)[0], FC, 16)
    shared["sinkb"] = np.ascontiguousarray(np.broadcast_to(f(a_sink)[0][None, :], (128, 8)))
    shared["gainA"] = np.ascontiguousarray(np.broadcast_to(f(a_out_gain)[0][None, :], (128, 1024)))
    shared["gainB"] = np.ascontiguousarray(np.broadcast_to(f(diff_subln_gain)[0][None, :], (128, 128)))
    lq = np.stack([f(diff_lq1)[0], f(diff_lk1)[0], f(diff_lq2)[0], f(diff_lk2)[0]], axis=0)
    shared["lqk"] = np.ascontiguousarray(np.broadcast_to(lq[None], (128, 4, 64)))
    in_maps = []
    for core in range(8):
        b, half = core // 2, core % 2
        xb = x[b]
        if half == 1:
            xb = xb[::-1]
        xTb = np.ascontiguousarray(xb.T.reshape(KC, 128, SEQ).transpose(1, 0, 2))
        m = dict(shared)
        m["xT"] = xTb
        m["cT"] = _vec16(f(c)[b])
        in_maps.append(m)
    return in_maps


_NC_CACHE = {}


def kernel(**inputs):
    if "nc" not in _NC_CACHE:
        _NC_CACHE["nc"] = build_program()
    nc = _NC_CACHE["nc"]
    in_maps = make_in_maps(**inputs)
    res = run_bass_kernel_spmd(nc, in_maps, core_ids=list(range(8)))
    out = np.empty((4, SEQ, D), dtype=np.float32)
    for core in range(8):
        b, half = core // 2, core % 2
        oT = np.asarray(res.results[core]["outT"])
        o = oT.transpose(2, 1, 0).reshape(OWN, D)
        if half == 0:
            out[b, :OWN] = o
        else:
            out[b, OWN:] = o[::-1]
    return out
```
